# Optimizing a Trainium2 kernel written in Bass

```python
import math
import jax, jax.numpy as jnp
from jax import lax
import numpy as np

D_MODEL = 1024
BATCH = 4
SEQ = 4096
DEPTH = 2
DEC_BATCH = 128
DEC_SEQ = 4
PAST_LEN = 2048
PAGE_SIZE = 128

N_A_LAYERS = DEPTH // 2
N_B_LAYERS = DEPTH - N_A_LAYERS
GLA_HEADS = 4
GLA_DK = D_MODEL // 2 // GLA_HEADS
GLA_DV = D_MODEL // GLA_HEADS
GLA_RANK = 16
GLA_TAU = 16.0
GLA_CHUNK = 64
FOX_HEADS = 16
FOX_HD = D_MODEL // FOX_HEADS
Q_BLOCK = 128
D_FF = 2816
N_SUB = 3
NORM_EPS = 1e-6

kernel_name = 'yoco_gla_fox_macaron_adaln_step'


def rmsnorm(x, g):
    xf = x.astype(jnp.float32)
    y = xf * lax.rsqrt(jnp.mean(xf * xf, axis=-1, keepdims=True) + NORM_EPS)
    return (y * g.astype(jnp.float32)).astype(x.dtype)


def modulate(x, g, shift, scale):
    return rmsnorm(x, g) * (1 + scale[:, None, :]) + shift[:, None, :]


def swiglu(u, w_up, w_down):
    a, b = jnp.split(u @ w_up, 2, axis=-1)
    return (jax.nn.silu(a) * b) @ w_down


def gla_chunk_scan(q, k, v, g, s0):
    B, L = q.shape[:2]
    C = math.gcd(L, GLA_CHUNK)
    n = L // C

    def to_chunks(t):
        return t.reshape(B, n, C, t.shape[2], t.shape[3]).transpose(1, 0, 3, 2, 4)

    qc, kc, vc, gc = to_chunks(q), to_chunks(k), to_chunks(v), to_chunks(g)
    causal = jnp.tril(jnp.ones((C, C), bool))[:, :, None]

    def step(S, inp):
        qi, ki, vi, gi = inp
        b = jnp.cumsum(gi, axis=2)
        b_last = b[:, :, -1:, :]
        o_inter = jnp.einsum('bhcd,bhde->bhce', qi * jnp.exp(b), S)
        diff = b[:, :, :, None, :] - b[:, :, None, :, :]
        decay = jnp.exp(jnp.where(causal, diff, -jnp.inf))
        att = jnp.einsum('bhid,bhjd,bhijd->bhij', qi, ki, decay)
        o = o_inter + jnp.einsum('bhij,bhje->bhie', att, vi)
        S = jnp.exp(b_last)[:, :, 0, :, None] * S + jnp.einsum('bhcd,bhce->bhde', ki * jnp.exp(b_last - b), vi)
        return S, o

    S, o = lax.scan(step, s0, (qc, kc, vc, gc))
    o = o.transpose(1, 0, 3, 2, 4).reshape(B, L, GLA_HEADS, GLA_DV)
    return o, S


def gla_mixer(u, s0, w_in, w_gate2, b_gate, g_out, w_out):
    B, L, _ = u.shape
    dk = GLA_HEADS * GLA_DK
    dv = GLA_HEADS * GLA_DV
    proj = u @ w_in
    q, k, v, r, glr = jnp.split(proj, [dk, 2 * dk, 2 * dk + dv, 2 * dk + 2 * dv], axis=-1)
    log_a = jax.nn.log_sigmoid((glr @ w_gate2 + b_gate).astype(jnp.float32)) / GLA_TAU
    f32 = jnp.float32
    o, S = gla_chunk_scan(
        q.astype(f32).reshape(B, L, GLA_HEADS, GLA_DK) * GLA_DK ** -0.5,
        k.astype(f32).reshape(B, L, GLA_HEADS, GLA_DK),
        v.astype(f32).reshape(B, L, GLA_HEADS, GLA_DV),
        log_a.reshape(B, L, GLA_HEADS, GLA_DK),
        s0.astype(f32))
    o = rmsnorm(o, g_out).reshape(B, L, dv).astype(u.dtype) * jax.nn.silu(r)
    return o @ w_out, S.astype(s0.dtype)


def shared_kv(h, shift, scale, g_kv, w_kvf, b_f, g_k):
    B, L, _ = h.shape
    dfox = FOX_HEADS * FOX_HD
    kvf = modulate(h, g_kv, shift, scale) @ w_kvf
    k, v, fl = jnp.split(kvf, [dfox, 2 * dfox], axis=-1)
    k = rmsnorm(k.reshape(B, L, FOX_HEADS, FOX_HD), g_k)
    v = v.reshape(B, L, FOX_HEADS, FOX_HD)
    logf = jax.nn.log_sigmoid((fl + b_f).astype(jnp.float32))
    return k, v, logf


def fox_prompt_attend(q, k, v, logf):
    B, L = q.shape[:2]
    Ft = jnp.cumsum(logf.astype(jnp.float32), axis=1).transpose(0, 2, 1)
    nb = L // Q_BLOCK
    qb = q.reshape(B, nb, Q_BLOCK, FOX_HEADS, FOX_HD).transpose(1, 0, 2, 3, 4)
    Fb = Ft.reshape(B, FOX_HEADS, nb, Q_BLOCK).transpose(2, 0, 1, 3)
    kpos = jnp.arange(L)
    scale = FOX_HD ** -0.5

    def block(args):
        i, qi, Fi = args
        qpos = i * Q_BLOCK + jnp.arange(Q_BLOCK)
        s = jnp.einsum('bqhd,bkhd->bhqk', qi, k).astype(jnp.float32) * scale
        s = s + (Fi[:, :, :, None] - Ft[:, :, None, :])
        s = jnp.where(kpos[None, :] <= qpos[:, None], s, -jnp.inf)
        p = jax.nn.softmax(s, axis=-1)
        return jnp.einsum('bhqk,bkhd->bqhd', p.astype(v.dtype), v)

    o = lax.map(block, (jnp.arange(nb), qb, Fb))
    return o.transpose(1, 0, 2, 3, 4).reshape(B, L, FOX_HEADS, FOX_HD)


def fox_paged_attend(q, k_new, v_new, logf_new, cache_k, cache_v, cache_logf, page_table):
    DB, T = q.shape[:2]
    kp = cache_k[page_table].reshape(DB, -1, FOX_HEADS, FOX_HD)
    vp = cache_v[page_table].reshape(DB, -1, FOX_HEADS, FOX_HD)
    lp = cache_logf[page_table].reshape(DB, -1, FOX_HEADS)
    P = kp.shape[1]
    F_past = jnp.cumsum(lp.astype(jnp.float32), axis=1)
    F_new = F_past[:, -1:, :] + jnp.cumsum(logf_new.astype(jnp.float32), axis=1)
    Fq = F_new.transpose(0, 2, 1)[:, :, :, None]
    scale = FOX_HD ** -0.5
    s_past = jnp.einsum('bqhd,bkhd->bhqk', q, kp).astype(jnp.float32) * scale \
        + Fq - F_past.transpose(0, 2, 1)[:, :, None, :]
    s_new = jnp.einsum('bqhd,bkhd->bhqk', q, k_new).astype(jnp.float32) * scale \
        + Fq - F_new.transpose(0, 2, 1)[:, :, None, :]
    s_new = jnp.where(jnp.tril(jnp.ones((T, T), bool)), s_new, -jnp.inf)
    p = jax.nn.softmax(jnp.concatenate([s_past, s_new], axis=-1), axis=-1).astype(vp.dtype)
    o = jnp.einsum('bhqk,bkhd->bqhd', p[..., :P], vp) \
        + jnp.einsum('bhqk,bkhd->bqhd', p[..., P:].astype(v_new.dtype), v_new)
    return o.astype(q.dtype)


def fox_mixer(u, kv, attend, w_qg, g_q, w_o):
    B, L, _ = u.shape
    k, v, logf = kv
    q, og = jnp.split(u @ w_qg, 2, axis=-1)
    q = rmsnorm(q.reshape(B, L, FOX_HEADS, FOX_HD), g_q)
    o = attend(q, k, v, logf).reshape(B, L, FOX_HEADS * FOX_HD) * jax.nn.sigmoid(og)
    return o @ w_o


def trunk(x, c, gla_init, attend, P):
    Bn = x.shape[0]
    sc = jax.nn.silu(c)
    h = x
    gla_states = []
    kv = None
    for layer in range(DEPTH):
        mod = (sc @ P['w_ada'][layer] + P['b_ada'][layer]).reshape(Bn, N_SUB, 3, D_MODEL)
        u = modulate(h, P['g_norm'][layer, 0], mod[:, 0, 0], mod[:, 0, 1])
        h = h + 0.5 * mod[:, 0, 2][:, None, :] * swiglu(u, P['w_ffn_up'][layer, 0], P['w_ffn_down'][layer, 0])
        u = modulate(h, P['g_norm'][layer, 1], mod[:, 1, 0], mod[:, 1, 1])
        if layer < N_A_LAYERS:
            a = layer
            mix, S = gla_mixer(u, gla_init[a], P['gla_w_in'][a], P['gla_w_gate2'][a],
                               P['gla_b_gate'][a], P['gla_g_out'][a], P['gla_w_out'][a])
            gla_states.append(S)
        else:
            b = layer - N_A_LAYERS
            mix = fox_mixer(u, kv, attend, P['fox_w_qg'][b], P['fox_g_q'][b], P['fox_w_o'][b])
        h = h + mod[:, 1, 2][:, None, :] * mix
        u = modulate(h, P['g_norm'][layer, 2], mod[:, 2, 0], mod[:, 2, 1])
        h = h + 0.5 * mod[:, 2, 2][:, None, :] * swiglu(u, P['w_ffn_up'][layer, 1], P['w_ffn_down'][layer, 1])
        if layer == N_A_LAYERS - 1:
            shift_kv, scale_kv = jnp.split(sc @ P['w_ada_kv'] + P['b_ada_kv'], 2, axis=-1)
            kv = shared_kv(h, shift_kv, scale_kv, P['g_kv'], P['w_kvf'], P['b_f'], P['g_k'])
    k, v, logf = kv
    return h, jnp.stack(gla_states), k, v, logf.astype(x.dtype)


def setup_inputs(seed: int = 0) -> dict:
    key = jax.random.key(seed)
    keys = iter(jax.random.split(key, 40))

    def nrm(shape, s):
        return jax.random.normal(next(keys), shape, jnp.float32) * s

    n_pages = PAST_LEN // PAGE_SIZE
    used = DEC_BATCH * n_pages
    n_phys = used + max(1, used // 4)
    dk = GLA_HEADS * GLA_DK
    dv = GLA_HEADS * GLA_DV
    dfox = FOX_HEADS * FOX_HD
    page_table = jax.random.permutation(next(keys), n_phys)[:used].reshape(DEC_BATCH, n_pages).astype(jnp.int32)
    return {
        'x_prompt': nrm((BATCH, SEQ, D_MODEL), 1.0),
        'x_sample': nrm((DEC_BATCH, DEC_SEQ, D_MODEL), 1.0),
        'state_gla': nrm((N_A_LAYERS, DEC_BATCH, GLA_HEADS, GLA_DK, GLA_DV), 0.5),
        'cache_k': nrm((n_phys, PAGE_SIZE, FOX_HEADS, FOX_HD), 1.0),
        'cache_v': nrm((n_phys, PAGE_SIZE, FOX_HEADS, FOX_HD), 1.0),
        'cache_logf': jax.nn.log_sigmoid(nrm((n_phys, PAGE_SIZE, FOX_HEADS), 1.0)),
        'page_table': page_table,
        'c_prompt': nrm((BATCH, D_MODEL), 1.0),
        'c_sample': nrm((DEC_BATCH, D_MODEL), 1.0),
        'w_ada': nrm((DEPTH, D_MODEL, N_SUB * 3 * D_MODEL), D_MODEL ** -0.5),
        'b_ada': nrm((DEPTH, N_SUB * 3 * D_MODEL), 0.02),
        'g_norm': 1.0 + nrm((DEPTH, N_SUB, D_MODEL), 0.02),
        'w_ffn_up': nrm((DEPTH, 2, D_MODEL, 2 * D_FF), D_MODEL ** -0.5),
        'w_ffn_down': nrm((DEPTH, 2, D_FF, D_MODEL), D_FF ** -0.5),
        'gla_w_in': nrm((N_A_LAYERS, D_MODEL, 2 * dk + 2 * dv + GLA_RANK), D_MODEL ** -0.5),
        'gla_w_gate2': nrm((N_A_LAYERS, GLA_RANK, dk), GLA_RANK ** -0.5),
        'gla_b_gate': nrm((N_A_LAYERS, dk), 0.1),
        'gla_g_out': 1.0 + nrm((N_A_LAYERS, GLA_DV), 0.02),
        'gla_w_out': nrm((N_A_LAYERS, dv, D_MODEL), dv ** -0.5),
        'w_ada_kv': nrm((D_MODEL, 2 * D_MODEL), D_MODEL ** -0.5),
        'b_ada_kv': nrm((2 * D_MODEL,), 0.02),
        'g_kv': 1.0 + nrm((D_MODEL,), 0.02),
        'w_kvf': nrm((D_MODEL, 2 * dfox + FOX_HEADS), D_MODEL ** -0.5),
        'b_f': nrm((FOX_HEADS,), 0.1),
        'g_k': 1.0 + nrm((FOX_HD,), 0.02),
        'fox_w_qg': nrm((N_B_LAYERS, D_MODEL, 2 * dfox), D_MODEL ** -0.5),
        'fox_g_q': 1.0 + nrm((N_B_LAYERS, FOX_HD), 0.02),
        'fox_w_o': nrm((N_B_LAYERS, dfox, D_MODEL), dfox ** -0.5),
    }


def reference(x_prompt, x_sample, state_gla, cache_k, cache_v, cache_logf, page_table, c_prompt, c_sample,
              w_ada, b_ada, g_norm, w_ffn_up, w_ffn_down,
              gla_w_in, gla_w_gate2, gla_b_gate, gla_g_out, gla_w_out,
              w_ada_kv, b_ada_kv, g_kv, w_kvf, b_f, g_k,
              fox_w_qg, fox_g_q, fox_w_o):
    P = dict(w_ada=w_ada, b_ada=b_ada, g_norm=g_norm, w_ffn_up=w_ffn_up, w_ffn_down=w_ffn_down,
             gla_w_in=gla_w_in, gla_w_gate2=gla_w_gate2, gla_b_gate=gla_b_gate,
             gla_g_out=gla_g_out, gla_w_out=gla_w_out,
             w_ada_kv=w_ada_kv, b_ada_kv=b_ada_kv, g_kv=g_kv, w_kvf=w_kvf, b_f=b_f, g_k=g_k,
             fox_w_qg=fox_w_qg, fox_g_q=fox_g_q, fox_w_o=fox_w_o)
    gla_zero = jnp.zeros((N_A_LAYERS, x_prompt.shape[0], GLA_HEADS, GLA_DK, GLA_DV), x_prompt.dtype)
    y_prompt, sg_prompt, k_prompt, v_prompt, lf_prompt = trunk(x_prompt, c_prompt, gla_zero, fox_prompt_attend, P)

    def paged_attend(q, k, v, lf):
        return fox_paged_attend(q, k, v, lf, cache_k, cache_v, cache_logf, page_table)

    y_sample, sg_sample, k_sample, v_sample, lf_sample = trunk(x_sample, c_sample, state_gla, paged_attend, P)
    return (y_prompt, y_sample, sg_prompt, k_prompt, v_prompt, lf_prompt, sg_sample, k_sample, v_sample, lf_sample)
```

```python
import contextlib
import numpy as np
import concourse.bass as bass
import concourse.mybir as mybir
from concourse.bass_utils import run_bass_kernel_spmd

F32 = mybir.dt.float32
BF16 = mybir.dt.bfloat16
I32 = mybir.dt.int32
AF = mybir.ActivationFunctionType
ALU = mybir.AluOpType
AX = mybir.AxisListType

D = 1024
KC = 8
DFF = 2816
JC = 22
NPR = 2048
NSB = 16
NS = 64
NT = NPR + NS
NPG = 16
EPS = 1e-6
BIG = 30000.0
N_CORES = 8
DEBUG = False
NOW = ''

C_IDENT, C_TRIN, C_TRUN, C_M01, C_NEGM, C_OBLK, C_ONES, C_TRINS, C_TRUNS, C_M01S, C_SHIFT, C_USTR, C_TRI1S, C_MB = range(14)
NCONST = 14


class Sched:
    ENG = ['pe', 'act', 'dve', 'pool', 'sp']

    def __init__(self, sems, dma_sems):
        self.sem = sems
        self.dsem = dma_sems
        self.prog = {e: [] for e in self.ENG}
        self.cnt = {e: 0 for e in self.ENG}
        self.duse = {q: [0] * len(v) for q, v in dma_sems.items()}
        self.dn = {q: 0 for q in dma_sems}
        self.waited = {e: {} for e in self.ENG}
        self.last_w = {}
        self.readers = {}

    def _need(self, eng, toks):
        need = {}
        for (sid, semh, val, owner, big) in toks:
            if owner == eng and (eng == 'pe' or big):
                continue
            if self.waited[eng].get(sid, 0) >= val:
                continue
            if need.get(sid, (None, 0))[1] < val:
                need[sid] = (semh, val)
        for sid, (semh, val) in need.items():
            self.waited[eng][sid] = val
        return list(need.values())

    def _deps(self, eng, reads, writes):
        toks = []
        for k in reads:
            toks.extend(self.last_w.get(k, {}).values())
        for k in writes:
            toks.extend(self.last_w.get(k, {}).values())
            toks.extend(self.readers.get(k, ()))
        return self._need(eng, toks)

    def _commit(self, tok, reads, writes):
        for k in reads:
            self.readers.setdefault(k, []).append(tok)
        for k in writes:
            self.last_w.setdefault(k, {})[tok[0]] = tok
            self.readers[k] = []

    def op(self, eng, fn, reads=(), writes=(), big=False):
        waits = self._deps(eng, reads, writes)
        self.cnt[eng] += 1
        semh = self.sem[eng]

        def run(e, waits=waits, fn=fn, semh=semh):
            for (s, v) in waits:
                e.wait_ge(s, v)
            fn(e).then_inc(semh, 1)
        self.prog[eng].append(run)
        self._commit(('E' + eng, semh, self.cnt[eng], eng, big), reads, writes)

    def dma(self, q, fn, reads=(), writes=()):
        waits = self._deps(q, reads, writes)
        slot = self.dn[q] % len(self.dsem[q])
        self.dn[q] += 1
        semh = self.dsem[q][slot]
        prev = self.duse[q][slot]
        sid = 'D%s%d' % (q, slot)
        if prev > 0 and self.waited[q].get(sid, 0) < 16 * prev:
            waits = waits + [(semh, 16 * prev)]
            self.waited[q][sid] = 16 * prev
        self.duse[q][slot] = prev + 1

        def run(e, waits=waits, fn=fn, semh=semh):
            for (s, v) in waits:
                e.wait_ge(s, v)
            fn(e).then_inc(semh, 16)
        self.prog[q].append(run)
        self._commit((sid, semh, 16 * (prev + 1), 'dma', False), reads, writes)

    def _all_tokens(self):
        toks = []
        for q, lst in self.dsem.items():
            for i, s in enumerate(lst):
                if self.duse[q][i] > 0:
                    toks.append(('D%s%d' % (q, i), s, 16 * self.duse[q][i], 'dma', False))
        for e in self.ENG:
            if self.cnt[e] > 0:
                toks.append(('E' + e, self.sem[e], self.cnt[e], e, True))
        return toks

    def barrier(self):
        toks = self._all_tokens()
        for e in self.ENG:
            waits = self._need(e, toks)
            if waits:
                def run(eng, waits=waits):
                    for (s, v) in waits:
                        eng.wait_ge(s, v)
                self.prog[e].append(run)

    def finish(self):
        waits = self._need('sp', self._all_tokens())

        def run(eng, waits=waits):
            for (s, v) in waits:
                eng.wait_ge(s, v)
        self.prog['sp'].append(run)

    def emit(self, block):
        prog = self.prog

        @block.tensor
        def _(e):
            for f in prog['pe']:
                f(e)

        @block.scalar
        def _(e):
            for f in prog['act']:
                f(e)

        @block.vector
        def _(e):
            for f in prog['dve']:
                f(e)

        @block.gpsimd
        def _(e):
            for f in prog['pool']:
                f(e)

        @block.sync
        def _(e):
            for f in prog['sp']:
                f(e)


def _consts():
    c = np.zeros((128, NCONST, 128), np.float32)
    j = np.arange(128)[:, None]
    i = np.arange(128)[None, :]
    c[:, C_IDENT] = (j == i)
    c[:, C_TRIN] = (j <= i) * (-1.0 / 16.0)
    c[:, C_TRUN] = (j > i) * (-1.0 / 16.0)
    c[:, C_M01] = (j <= i)
    c[:, C_NEGM] = (j > i) * (-BIG)
    c[:, C_OBLK] = ((j // 64) == (i // 64)) * (1.0 / 64.0)
    c[:, C_ONES] = 1.0
    same = ((j // 4) == (i // 4)) & (j < 64) & (i < 64)
    c[:, C_TRINS] = (same & (j <= i)) * (-1.0 / 16.0)
    c[:, C_TRUNS] = (same & (j > i)) * (-1.0 / 16.0)
    c[:, C_M01S] = (same & (j <= i))
    c[:, C_SHIFT] = ((j == i + 64) & (i < 64))
    c[:, C_USTR] = (j > i)
    c[:, C_TRI1S] = (same & (j <= i))
    c[:, C_MB] = ((j // 4) == i) & (j < 64) & (i < 16)
    return c


def _maskn():
    kb = (np.arange(64) // 4)[:, None, None, None]
    ks = (np.arange(64) % 4)[:, None, None, None]
    b = np.arange(16)[None, :, None, None]
    t = np.arange(4)[None, None, None, :]
    ok = (kb == b) & (ks <= t)
    m = np.where(ok, 0.0, -BIG).astype(np.float32)
    return np.ascontiguousarray(np.broadcast_to(m, (64, 16, 16, 4))).reshape(64, 1024)


def build_program(n_phys):
    nc = bass.Bass("TRN2", target_bir_lowering=False)
    NROWS = n_phys * 128

    def din(name, shape, dt=F32):
        return nc.dram_tensor(name, list(shape), dt, kind="ExternalInput").ap()

    def dout(name, shape, dt=F32):
        return nc.dram_tensor(name, list(shape), dt, kind="ExternalOutput").ap()

    def dscr(name, shape, dt):
        return nc.dram_tensor(name, list(shape), dt, kind="Internal").ap()

    xo = din("xo", [D, NPR]); xp = din("xp", [D, NPR]); xs = din("xs", [D, NS])
    cT = din("cT", [D, 17])
    st0 = din("st0", [NSB, 4, 128, 256])
    ckv = din("ckv", [NROWS, 2 * D]); clf = din("clf", [NROWS, 16])
    ptab = din("ptab", [1, NSB * NPG], I32)
    iota = din("iota", [128, 1], I32)
    flg = din("flg", [128, 2])
    w_ada = din("w_ada", [2, D, 9 * D]); b_adaT = din("b_adaT", [128, 2, 72])
    g_normT = din("g_normT", [128, 2, 3, 8])
    w_up = din("w_up", [2, 2, D, 2 * DFF]); w_dn = din("w_dn", [2, 2, DFF, D])
    w_in = din("w_in", [D, 3088]); wg2b = din("wg2b", [17, 512]); goutT = din("goutT", [128, 2])
    w_out = din("w_out", [D, D])
    w_akv = din("w_akv", [D, 2 * D]); b_akvT = din("b_akvT", [128, 16]); g_kvT = din("g_kvT", [128, 8])
    w_kvf = din("w_kvf", [D, 2064]); bfb = din("bfb", [128, 16]); gk2 = din("gk2", [128, 1])
    w_qg = din("w_qg", [D, 2 * D]); gq2 = din("gq2", [128, 1]); w_o = din("w_o", [D, D])
    cpack = din("cpack", [128, NCONST, 128]); maskn = din("maskn", [64, 1024])

    yo = dout("yo", [D, NPR]); ys = dout("ys", [D, NS])
    sgp = dout("sgp", [4, 128, 256])
    ko = dout("ko", [D, NPR]); vo = dout("vo", [NPR, D]); lfo = dout("lfo", [NPR, 16])
    sgs = dout("sgs", [NSB, 4, 128, 256])
    kso = dout("kso", [D, NS]); vso = dout("vso", [NS, D]); lfso = dout("lfso", [NS, 16])

    KTD = dscr("KTD", [16, 70, 2 * NPR], BF16)
    VD = dscr("VD", [2 * NPR, 16, 128], BF16)
    QD = dscr("QD", [16, 70, NPR], BF16)
    OD = dscr("OD", [D, NPR], BF16)

    with contextlib.ExitStack() as st:
        E = st.enter_context

        def sb(name, shape, dt=F32):
            return E(nc.sbuf_tensor(name, list(shape), dt))

        H = sb("H", [128, KC, NT])
        U = sb("U", [128, KC, 1088], BF16)
        WB = [sb("WB%d" % i, [128, 4096], BF16) for i in range(4)]
        MOD = sb("MOD", [128, 2, 72, 17])
        MODKV = sb("MODKV", [128, 16, 17])
        CF = sb("CF", [128, NCONST, 128])
        CB = sb("CB", [128, 6, 128], BF16)
        TMP = sb("TMP", [128, 2, 512])
        RSTD = sb("RSTD", [128, 512])
        SST = sb("SST", [128, 4, 256])
        SBF = sb("SBF", [128, 4, 256], BF16)
        LFT = sb("LFT", [128, 33, 16])
        BADA = sb("BADA", [128, 2, 72]); GN = sb("GN", [128, 2, 3, 8]); BAKV = sb("BAKV", [128, 16]); GKV = sb("GKV", [128, 8])
        GOUT = sb("GOUT", [128, 2]); BFB = sb("BFB", [128, 16]); GK = sb("GK", [128, 1]); GQ = sb("GQ", [128, 1])
        FLG = sb("FLG", [128, 2]); EPSC = sb("EPSC", [128, 2]); WG2 = sb("WG2", [17, 512])
        CTS = sb("CTS", [128, KC, 17]); SCT = sb("SCT", [128, KC, 17], BF16)
        PS = [E(nc.psum_tensor("PS%d" % i, [128, 512], F32)) for i in range(8)]
        RBYTES = 53000
        REG = sb("REG", [128, RBYTES // 4], F32)

        def carve(off, shape, dt):
            assert off % 4 == 0
            n = int(np.prod(shape[1:]))
            nb = n * (4 if dt == F32 else 2)
            assert off + nb <= RBYTES, (off, shape)
            v = REG[:, off // 4:(off + nb + 3) // 4]
            if dt != F32:
                v = v.bitcast(dt)[:, 0:n]
            v = v[0:shape[0], :]
            if len(shape) == 3:
                v = v.rearrange("p (a b) -> p a b", a=shape[1])
            elif len(shape) == 4:
                v = v.rearrange("p (a b c) -> p a b c", a=shape[1], b=shape[2])
            return v

        sems = {e: E(nc.semaphore("s_" + e)) for e in Sched.ENG}
        dsems = {q: [E(nc.semaphore("d_%s%d" % (q, i))) for i in range(8)] for q in ['sp', 'pool']}
        block = E(nc.Block())
        S = Sched(sems, dsems)

        IDB, O1024, OBLKB, O256, NEGMB, ONEB = [CB[:, i, :] for i in range(6)]

        def cf(i):
            return CF[:, i, :]

        def PK(i):
            return ('ps', i)

        def dbg(name, src_ap, shape, reads):
            if not DEBUG:
                return
            t_ = dout("dbg_" + name, shape)
            if len(shape) == 2:
                dst = t_[:, :]
            else:
                dst = t_[:, :, :]
            S.dma('sp', lambda e: e.dma_start(out=dst, in_=src_ap), reads=reads)

        def dbg_h(stage):
            dbg("hs_" + stage, H[:, :, NPR:NT], [128, KC, NS], [('H', 4)])
            dbg("hp_" + stage, H[:, :, 0:512], [128, KC, 512], [('H', 0)])

        def pe_group(mms, reads, writes):
            def fn(e, mms=mms):
                r = None
                for (o, l, rh, s0, s1) in mms:
                    r = e.matmul(o, l, rh, start=s0, stop=s1)
                return r
            S.op('pe', fn, reads=reads, writes=writes)

        def act(out, in_, func, reads, writes, bias=None, scale=None):
            kw = {}
            if bias is not None:
                kw['bias'] = bias
            if scale is not None:
                kw['scale'] = scale
            big = int(np.prod(out.shape[1:])) >= 256
            S.op('act', lambda e: e.activation(out=out, in_=in_, func=func, **kw), reads=reads, writes=writes, big=big)

        def dve(fn, reads, writes, big=False):
            S.op('dve', fn, reads=reads, writes=writes, big=big)

        wbi = [0]

        def wload(parts):
            i = wbi[0] % len(WB)
            wbi[0] += 1
            buf = WB[i]
            key = ('WB', i)
            for (dv, src) in parts:
                if NOW == 'noweights' and wbi[0] > 8:
                    continue
                S.dma('pool', lambda e, dv=dv, src=src, buf=buf: e.dma_start(out=dv(buf), in_=src), writes=[key])
            return buf, key

        def wslab(w2d, c0, ncols):
            src = w2d.rearrange("(k p) n -> p k n", p=128)[:, :, c0:c0 + ncols]
            vf = lambda buf, ncols=ncols: buf[:, 0:KC * ncols].rearrange("p (k n) -> p k n", k=KC)
            buf, key = wload([(vf, src)])
            return vf(buf), key

        def rstd_from_ps(ps_ap, n, out_ap, rkeys, wkeys):
            act(out_ap, ps_ap, AF.Ln, rkeys, wkeys, bias=EPSC[:ps_ap.shape[0], 0:1], scale=1.0)
            act(out_ap, out_ap, AF.Exp, wkeys, wkeys, scale=-0.5)

        PT_TILES = [(i * 512, 512, False, i) for i in range(4)]
        S_TILE = (NPR, NS, True, 4)

        def hk(t):
            return ('H', t[3])

        def modcol(l, sub, j, k):
            if l == 'kv':
                return MODKV[:, j * 8 + k, :]
            return MOD[:, l, (sub * 3 + j) * 8 + k, :]

        def norm_mod(t, l, sub, ucol, ukey):
            c0, n, samp, _ = t
            hv = H[:, :, c0:c0 + n]
            uv = U[:, :, ucol:ucol + n]
            act(uv, hv, AF.Square, [hk(t)], [ukey])
            pe_group([(PS[7][:, 0:n], O1024, U[:, k, ucol:ucol + n], k == 0, k == KC - 1) for k in range(KC)],
                     [ukey], [PK(7)])
            rstd_from_ps(PS[7][:, 0:n], n, RSTD[:, 0:n], [PK(7)], ['RSTD'])
            if not samp:
                for k in range(KC):
                    tk = ('TMP', k % 2)
                    dve(lambda e, k=k: e.scalar_tensor_tensor(out=TMP[:, k % 2, 0:n], in0=H[:, k, c0:c0 + n],
                                                              scalar=modcol(l, sub, 1, k)[:, 0:1], in1=RSTD[:, 0:n],
                                                              op0=ALU.mult, op1=ALU.mult),
                        [hk(t), 'RSTD', 'MOD'], [tk], big=True)
                    act(U[:, k, ucol:ucol + n], TMP[:, k % 2, 0:n], AF.Identity, [tk, 'MOD'], [ukey],
                        bias=modcol(l, sub, 0, k)[:, 0:1], scale=1.0)
            else:
                for k in range(KC):
                    tk = ('TMP', k % 2)
                    tv = TMP[:, k % 2, 0:n]
                    a_b = modcol(l, sub, 1, k)[:, 1:17].unsqueeze(2).to_broadcast([128, NSB, 4])
                    b_b = modcol(l, sub, 0, k)[:, 1:17].unsqueeze(2).to_broadcast([128, NSB, 4])
                    dve(lambda e, k=k, tv=tv: e.tensor_tensor(out=tv, in0=H[:, k, c0:c0 + n], in1=RSTD[:, 0:n], op=ALU.mult),
                        [hk(t), 'RSTD'], [tk])
                    dve(lambda e, tv=tv, a_b=a_b: e.tensor_tensor(out=tv.rearrange("p (b t) -> p b t", t=4),
                                                                  in0=tv.rearrange("p (b t) -> p b t", t=4), in1=a_b, op=ALU.mult),
                        [tk, 'MOD'], [tk])
                    dve(lambda e, k=k, tv=tv, b_b=b_b: e.tensor_tensor(
                        out=U[:, k, ucol:ucol + n].rearrange("p (b t) -> p b t", t=4),
                        in0=tv.rearrange("p (b t) -> p b t", t=4), in1=b_b, op=ALU.add),
                        [tk, 'MOD'], [ukey])

        def resid_add(t, l, sub, m, ps_ap, pskey):
            c0, n, samp, _ = t
            if not samp:
                dve(lambda e: e.scalar_tensor_tensor(out=H[:, m, c0:c0 + n], in0=ps_ap, scalar=modcol(l, sub, 2, m)[:, 0:1],
                                                     in1=H[:, m, c0:c0 + n], op0=ALU.mult, op1=ALU.add),
                    [pskey, hk(t), 'MOD'], [hk(t)], big=True)
            else:
                g_b = modcol(l, sub, 2, m)[:, 1:17].unsqueeze(2).to_broadcast([128, NSB, 4])
                tk = ('TMP', 0)
                dve(lambda e: e.tensor_tensor(out=TMP[:, 0, 0:n].rearrange("p (b t) -> p b t", t=4),
                                              in0=ps_ap.rearrange("p (b t) -> p b t", t=4), in1=g_b, op=ALU.mult),
                    [pskey, 'MOD'], [tk])
                dve(lambda e: e.tensor_tensor(out=H[:, m, c0:c0 + n], in0=H[:, m, c0:c0 + n], in1=TMP[:, 0, 0:n], op=ALU.add),
                    [tk, hk(t)], [hk(t)])

        def ld(dst, src, key, q='sp'):
            S.dma(q, lambda e: e.dma_start(out=dst, in_=src), writes=[key])

        ld(CF[:], cpack[:, :, :], 'CF')
        ld(BADA[:], b_adaT[:, :, :], 'small'); ld(GN[:], g_normT[:, :, :, :], 'small')
        ld(BAKV[:], b_akvT[:, :], 'small'); ld(GKV[:], g_kvT[:, :], 'small'); ld(GOUT[:], goutT[:, :], 'small')
        ld(BFB[:], bfb[:, :], 'small'); ld(GK[:], gk2[:, :], 'small'); ld(GQ[:], gq2[:, :], 'small')
        ld(FLG[:], flg[:, :], 'small'); ld(WG2[:], wg2b[:, :], 'small')
        ld(CTS[:], cT.rearrange("(k p) n -> p k n", p=128), 'CTS')
        dve(lambda e: e.memset(EPSC[:, 0:1], EPS), [], ['EPSC'])
        dve(lambda e: e.memset(EPSC[:, 1:2], 1.0), ['EPSC'], ['EPSC'])
        dve(lambda e: e.tensor_copy(out=IDB, in_=cf(C_IDENT)), ['CF'], ['CB'])
        dve(lambda e: e.tensor_scalar(out=O1024, in0=cf(C_ONES), scalar1=1.0 / 1024.0, scalar2=None, op0=ALU.mult), ['CF'], ['CB'])
        dve(lambda e: e.tensor_copy(out=OBLKB, in_=cf(C_OBLK)), ['CF'], ['CB'])
        dve(lambda e: e.tensor_scalar(out=O256, in0=cf(C_ONES), scalar1=1.0 / 256.0, scalar2=None, op0=ALU.mult), ['CF'], ['CB'])
        dve(lambda e: e.tensor_copy(out=NEGMB, in_=cf(C_NEGM)), ['CF'], ['CB'])
        dve(lambda e: e.tensor_copy(out=ONEB, in_=cf(C_ONES)), ['CF'], ['CB'])
        dve(lambda e: e.tensor_scalar(out=GQ[:], in0=GQ[:], scalar1=0.125, scalar2=None, op0=ALU.mult), ['small'], ['small'])
        S.barrier()

        act(SCT[:], CTS[:], AF.Silu, ['CTS'], ['SCT'])
        for l in range(2):
            for s6 in range(3):
                for sl in range(6):
                    slab = s6 * 6 + sl
                    wv, wk = wslab(w_ada[l], slab * 512, 512)
                    mms = []
                    for f4 in range(4):
                        fc = sl * 4 + f4
                        for k in range(KC):
                            mms.append((PS[s6][:, fc * 17:(fc + 1) * 17], wv[:, k, f4 * 128:(f4 + 1) * 128], SCT[:, k, :], k == 0, k == KC - 1))
                    pe_group(mms, [wk, 'SCT'], [PK(s6)])
                fc0 = s6 * 24
                dve(lambda e, l=l, s6=s6, fc0=fc0: e.tensor_tensor(
                    out=MOD[:, l, fc0:fc0 + 24, :], in0=PS[s6][:, 0:24 * 17].rearrange("p (f s) -> p f s", s=17),
                    in1=BADA[:, l, fc0:fc0 + 24].unsqueeze(2).to_broadcast([128, 24, 17]), op=ALU.add),
                    [PK(s6), 'small'], ['MOD'])
        for sl in range(4):
            wv, wk = wslab(w_akv, sl * 512, 512)
            mms = []
            for f4 in range(4):
                fc = sl * 4 + f4
                for k in range(KC):
                    mms.append((PS[3][:, fc * 17:(fc + 1) * 17], wv[:, k, f4 * 128:(f4 + 1) * 128], SCT[:, k, :], k == 0, k == KC - 1))
            pe_group(mms, [wk, 'SCT'], [PK(3)])
        dve(lambda e: e.tensor_tensor(out=MODKV[:], in0=PS[3][:, 0:16 * 17].rearrange("p (f s) -> p f s", s=17),
                                      in1=BAKV[:].unsqueeze(2).to_broadcast([128, 16, 17]), op=ALU.add), [PK(3), 'small'], ['MOD'])
        for l in range(2):
            for sub in range(3):
                f0 = (sub * 3 + 1) * 8
                dve(lambda e, l=l, sub=sub, f0=f0: e.scalar_tensor_tensor(
                    out=MOD[:, l, f0:f0 + 8, :], in0=MOD[:, l, f0:f0 + 8, :], scalar=1.0,
                    in1=GN[:, l, sub, :].unsqueeze(2).to_broadcast([128, 8, 17]), op0=ALU.add, op1=ALU.mult), ['MOD', 'small'], ['MOD'])
                if sub != 1:
                    g0 = (sub * 3 + 2) * 8
                    dve(lambda e, l=l, g0=g0: e.tensor_scalar(out=MOD[:, l, g0:g0 + 8, :], in0=MOD[:, l, g0:g0 + 8, :],
                                                              scalar1=0.5, scalar2=None, op0=ALU.mult), ['MOD'], ['MOD'])
        dve(lambda e: e.scalar_tensor_tensor(out=MODKV[:, 8:16, :], in0=MODKV[:, 8:16, :], scalar=1.0,
                                             in1=GKV[:].unsqueeze(2).to_broadcast([128, 8, 17]), op0=ALU.add, op1=ALU.mult),
            ['MOD', 'small'], ['MOD'])
        S.barrier()

        def ffn(l, which, sub, tiles):
            if True:
                HID = carve(0, [128, JC, 1088], BF16)
                SA = carve(JC * 1088 * 2, [128, 2, 512], F32)
                ucols = []
                uc = 0
                for t in tiles:
                    ucols.append(uc)
                    norm_mod(t, l, sub, uc, ('U', t[3]))
                    uc += t[1]
                wu = w_up[l, which]
                wd = w_dn[l, which]
                pi = [0]
                for j2 in range(JC // 2):
                    srca = wu.rearrange("(k p) n -> p k n", p=128)[:, :, j2 * 256:(j2 + 1) * 256]
                    srcb = wu.rearrange("(k p) n -> p k n", p=128)[:, :, DFF + j2 * 256:DFF + (j2 + 1) * 256]
                    va = lambda buf: buf[:, 0:2048].rearrange("p (k n) -> p k n", k=KC)
                    vb = lambda buf: buf[:, 2048:4096].rearrange("p (k n) -> p k n", k=KC)
                    buf, wk = wload([(va, srca), (vb, srcb)])
                    wa, wbb = va(buf), vb(buf)
                    for jj in range(2):
                        j = j2 * 2 + jj
                        for ti, t in enumerate(tiles):
                            n = t[1]
                            u0 = ucols[ti]
                            pa = pi[0] % 3
                            pb = 3 + pi[0] % 3
                            pi[0] += 1
                            pe_group([(PS[pa][:, 0:n], wa[:, k, jj * 128:(jj + 1) * 128], U[:, k, u0:u0 + n], k == 0, k == KC - 1) for k in range(KC)],
                                     [wk, ('U', t[3])], [PK(pa)])
                            pe_group([(PS[pb][:, 0:n], wbb[:, k, jj * 128:(jj + 1) * 128], U[:, k, u0:u0 + n], k == 0, k == KC - 1) for k in range(KC)],
                                     [wk, ('U', t[3])], [PK(pb)])
                            sk = ('SA', pa % 2)
                            act(SA[:, pa % 2, 0:n], PS[pa][:, 0:n], AF.Silu, [PK(pa)], [sk])
                            dve(lambda e, pa=pa, pb=pb, j=j, u0=u0, n=n: e.tensor_tensor(
                                out=HID[:, j, u0:u0 + n], in0=SA[:, pa % 2, 0:n], in1=PS[pb][:, 0:n], op=ALU.mult),
                                [sk, PK(pb)], [('HID', t[3])], big=(n >= 256))
                for m in range(KC):
                    src = wd.rearrange("(j p) n -> p j n", p=128)[:, :, m * 128:(m + 1) * 128]
                    vd = lambda buf: buf[:, 0:JC * 128].rearrange("p (j n) -> p j n", j=JC)
                    buf, wk = wload([(vd, src)])
                    wdv = vd(buf)
                    for ti, t in enumerate(tiles):
                        n = t[1]
                        u0 = ucols[ti]
                        pd = 6 + pi[0] % 2
                        pi[0] += 1
                        pe_group([(PS[pd][:, 0:n], wdv[:, j, :], HID[:, j, u0:u0 + n], j == 0, j == JC - 1) for j in range(JC)],
                                 [wk, ('HID', t[3])], [PK(pd)])
                        resid_add(t, l, sub, m, PS[pd][:, 0:n], PK(pd))
                S.barrier()

        class RA:
            def __init__(self):
                self.off = 0

            def get(self, shape, dt):
                n = int(np.prod(shape[1:])) * (4 if dt == F32 else 2)
                n = (n + 3) // 4 * 4
                v = carve(self.off, shape, dt)
                self.off += n
                return v

        def gla_tile(t, bar=True):
            c0, n, samp, idx = t
            C = 64 if samp else 128
            nblk = n // C
            ra = RA()
            QT = ra.get([128, 4, n], BF16); KT = ra.get([128, 4, n], BF16); SR = ra.get([128, 8, n], BF16)
            KTMA = ra.get([C, nblk, 512], BF16); VTMA = ra.get([C, nblk, 1024], BF16)
            GP = ra.get([C, 512], F32); EQ = ra.get([128, 4, C], F32); EK = ra.get([128, 4, C], F32)
            QS = ra.get([128, 4, C], BF16); KS = ra.get([128, 4, C], BF16); KH = ra.get([C, 512], BF16)
            ATM = ra.get([C, 4, C], BF16); RSO = ra.get([128, 4, C], F32); TO = ra.get([128, 8, C], F32)
            EB = ra.get([C, 512], F32); SQO = ra.get([128, 8, C], BF16)
            GLRT = ra.get([17, 512], F32)
            if samp:
                VBD = ra.get([64, 16, 256], BF16)
                S0B = [ra.get([128, 2, 4, 256], F32) for _ in range(2)]
                S0BF = [ra.get([128, 2, 4, 256], BF16) for _ in range(2)]
            ukey = ('U', idx)
            norm_mod(t, 0, 1, 0, ukey)
            pr = [0]

            def nextp():
                p = pr[0] % 4
                pr[0] += 1
                return p

            def fm_proj(wv, wk, cc, dst, func=AF.Copy, dkey=None):
                p = nextp()
                pe_group([(PS[p][:, 0:n], wv[:, k, cc * 128:(cc + 1) * 128], U[:, k, 0:n], k == 0, k == KC - 1) for k in range(KC)],
                         [wk, ukey], [PK(p)])
                act(dst, PS[p][:, 0:n], func, [PK(p)], [dkey])

            def tm_proj(wv, wk, b, dst, dkey):
                p = nextp()
                pe_group([(PS[p][0:C, 0:512], U[:, k, b * C:(b + 1) * C], wv[:, k, 0:512], k == 0, k == KC - 1) for k in range(KC)],
                         [wk, ukey], [PK(p)])
                dve(lambda e, p=p, dst=dst: e.tensor_copy(out=dst, in_=PS[p][0:C, 0:512]), [PK(p)], [dkey], big=True)

            wv, wk = wslab(w_in, 0, 512)
            for h in range(4):
                fm_proj(wv, wk, h, QT[:, h, :], dkey='QT')
            wv, wk = wslab(w_in, 512, 512)
            for h in range(4):
                fm_proj(wv, wk, h, KT[:, h, :], dkey='KT')
            for b in range(nblk):
                tm_proj(wv, wk, b, KTMA[:, b, :], 'KTMA')
            for half in range(2):
                wv, wk = wslab(w_in, 1024 + half * 512, 512)
                for b in range(nblk):
                    tm_proj(wv, wk, b, VTMA[:, b, half * 512:(half + 1) * 512], 'VTMA')
            for half in range(2):
                wv, wk = wslab(w_in, 2048 + half * 512, 512)
                for cc in range(4):
                    c8 = half * 4 + cc
                    fm_proj(wv, wk, cc, SR[:, c8, :], func=AF.Silu, dkey='SR')
            for ec in range(2):
                dve(lambda e, ec=ec: e.tensor_scalar(out=SR[:, ec::2, :], in0=SR[:, ec::2, :], scalar1=GOUT[:, ec:ec + 1], scalar2=None, op0=ALU.mult),
                    ['SR', 'small'], ['SR'])
            wv, wk = wslab(w_in, 3072, 16)
            p = nextp()
            dve(lambda e: e.memset(GLRT[:, 0:n], 1.0), [], ['GLRT'])
            pe_group([(PS[p][0:16, 0:n], wv[:, k, 0:16], U[:, k, 0:n], k == 0, k == KC - 1) for k in range(KC)], [wk, ukey], [PK(p)])
            act(GLRT[0:16, 0:n], PS[p][0:16, 0:n], AF.Copy, [PK(p)], ['GLRT'])

            TRIN = cf(C_TRINS if samp else C_TRIN)[0:C, 0:C]
            TRUN = cf(C_TRUNS if samp else C_TRUN)[0:C, 0:C]
            M01 = cf(C_M01S if samp else C_M01)[0:C, 0:C]
            for b in range(nblk):
                bs = slice(b * C, (b + 1) * C)
                pe_group([(PS[4][0:C, 0:512], GLRT[0:17, bs], WG2[0:17, 0:512], True, True)], ['GLRT', 'small'], [PK(4)])
                act(GP[:, :], PS[4][0:C, 0:512], AF.Exp, [PK(4)], ['GP'], scale=-1.0)
                act(GP[:, :], GP[:, :], AF.Ln, ['GP'], ['GP'], bias=EPSC[0:C, 1:2], scale=1.0)
                pe_group([(PS[5][:, h * C:(h + 1) * C], GP[:, h * 128:(h + 1) * 128], TRIN, True, True) for h in range(4)], ['GP', 'CF'], [PK(5)])
                pe_group([(PS[6][0:C, 0:512], TRUN, GP[:, :], True, True)], ['GP', 'CF'], [PK(6)])
                psb = PS[5][:, 0:4 * C].rearrange("p (h c) -> p h c", h=4)
                act(EQ[:], psb, AF.Exp, [PK(5)], ['EQ'])
                act(EK[:], psb, AF.Exp, [PK(5)], ['EK'], scale=-1.0)
                act(EB[:, :], PS[6][0:C, 0:512], AF.Exp, [PK(6)], ['EB'])
                dve(lambda e, bs=bs: e.scalar_tensor_tensor(out=QS[:], in0=QT[:, :, bs], scalar=128.0 ** -0.5, in1=EQ[:], op0=ALU.mult, op1=ALU.mult),
                    ['QT', 'EQ'], ['QS'])
                dve(lambda e, bs=bs: e.tensor_tensor(out=KS[:], in0=KT[:, :, bs], in1=EK[:], op=ALU.mult), ['KT', 'EK'], ['KS'])
                dve(lambda e, b=b: e.tensor_tensor(out=KH[:, :], in0=KTMA[:, b, :], in1=EB[:, :], op=ALU.mult), ['KTMA', 'EB'], ['KH'])
                pe_group([(PS[4][0:C, h * C:(h + 1) * C], KS[:, h, :], QS[:, h, :], True, True) for h in range(4)], ['KS', 'QS'], [PK(4)])
                dve(lambda e: e.tensor_tensor(out=ATM[:], in0=PS[4][0:C, 0:4 * C].rearrange("p (h c) -> p h c", h=4),
                                              in1=M01.unsqueeze(1).to_broadcast([C, 4, C]), op=ALU.mult), [PK(4), 'CF'], ['ATM'])

                def obank(hec):
                    if C == 128:
                        return PS[hec // 4][:, (hec % 4) * C:(hec % 4 + 1) * C]
                    return PS[0][:, hec * C:(hec + 1) * C]
                obanks = [PK(0), PK(1)] if C == 128 else [PK(0)]
                mms = []
                for h in range(4):
                    for ec in range(2):
                        hec = h * 2 + ec
                        mms.append((obank(hec), VTMA[:, b, h * 256 + ec * 128:h * 256 + (ec + 1) * 128], ATM[:, h, :], (not samp) or hec == 0, False))
                        if not samp:
                            mms.append((obank(hec), SBF[:, h, ec * 128:(ec + 1) * 128], QS[:, h, :], False, True))
                pe_group(mms, ['VTMA', 'ATM', 'SBF', 'QS'], obanks)
                if samp and DEBUG:
                    DB1 = TO[:].rearrange("p a c -> p (a c)"); DB2 = DB1; DB3 = RSO[:].rearrange("p a c -> p (a c)")[0:64, :]
                    dve(lambda e: e.tensor_copy(out=DB1[:, :], in_=PS[0][:, 0:512]), [PK(0)], ['TO'])
                    dbg("ot_intra", DB1[:, :], [128, 512], ['TO'])
                    dve(lambda e: e.tensor_copy(out=DB3[:, :], in_=ATM[:].rearrange("p h c -> p (h c)")), ['ATM'], ['RSO'])
                    dbg("atm", DB3[:, :], [64, 256], ['RSO'])
                if not samp:
                    pe_group([(PS[2 + h // 2][:, (h % 2) * 256:(h % 2 + 1) * 256], KH[:, h * 128:(h + 1) * 128], VTMA[:, b, h * 256:(h + 1) * 256], True, True)
                              for h in range(4)], ['KH', 'VTMA'], [PK(2), PK(3)])
                    for h in range(4):
                        dve(lambda e, h=h: e.scalar_tensor_tensor(out=SST[:, h, :], in0=SST[:, h, :], scalar=EQ[:, h, C - 1:C],
                                                                  in1=PS[2 + h // 2][:, (h % 2) * 256:(h % 2 + 1) * 256], op0=ALU.mult, op1=ALU.add),
                            ['SST', 'EQ', PK(2 + h // 2)], ['SST'])
                    act(SBF[:], SST[:], AF.Copy, ['SST'], ['SBF'])
                else:
                    for bp in range(NSB // 2):
                        sbuf = S0B[bp % 2]
                        skey = ('S0B', bp % 2)
                        S.dma('sp', lambda e, bp=bp, sbuf=sbuf: e.dma_start(out=sbuf[:, 0, :, :], in_=st0[2 * bp].rearrange("h d e -> d h e")), writes=[skey])
                        S.dma('sp', lambda e, bp=bp, sbuf=sbuf: e.dma_start(out=sbuf[:, 1, :, :], in_=st0[2 * bp + 1].rearrange("h d e -> d h e")), writes=[skey])
                        sbf = S0BF[bp % 2]
                        sbk = ('S0BF', bp % 2)
                        act(sbf[:].rearrange("p a h e -> p (a h e)"), sbuf[:].rearrange("p a h e -> p (a h e)"), AF.Copy, [skey], [sbk])
                        mms = []
                        for bb in range(2):
                            sq = 2 * bp + bb
                            for h in range(4):
                                for ec in range(2):
                                    hec = h * 2 + ec
                                    last = (bp == NSB // 2 - 1 and bb == 1)
                                    mms.append((PS[0][:, hec * C + 4 * sq:hec * C + 4 * sq + 4], sbf[:, bb, h, ec * 128:(ec + 1) * 128],
                                                QS[:, h, 4 * sq:4 * sq + 4], False, last))
                        pe_group(mms, [sbk, 'QS'], [PK(0)])
                        for h in range(4):
                            vk = 'VBD'
                            if bp == 0 or True:
                                pass
                            dve(lambda e, h=h, bp=bp: e.tensor_tensor(
                                out=VBD[:, 0:2, :], in0=VTMA[:, 0, h * 256:(h + 1) * 256].unsqueeze(1).to_broadcast([64, 2, 256]),
                                in1=cf(C_MB)[0:64, 2 * bp:2 * bp + 2].unsqueeze(2).to_broadcast([64, 2, 256]), op=ALU.mult),
                                ['VTMA', 'CF'], [vk])
                            pq = 2 + (bp * 4 + h) % 2
                            pe_group([(PS[pq][:, 0:512], KH[:, h * 128:(h + 1) * 128], VBD[:, 0:2, :].rearrange("p a b -> p (a b)"), True, True)],
                                     ['KH', vk], [PK(pq)])
                            for bb in range(2):
                                sq = 2 * bp + bb
                                dve(lambda e, h=h, bb=bb, sq=sq, pq=pq, sbuf=sbuf: e.scalar_tensor_tensor(
                                    out=sbuf[:, bb, h, :], in0=sbuf[:, bb, h, :], scalar=EQ[:, h, 4 * sq + 3:4 * sq + 4],
                                    in1=PS[pq][:, bb * 256:(bb + 1) * 256], op0=ALU.mult, op1=ALU.add), [skey, 'EQ', PK(pq)], [skey])
                        for bb in range(2):
                            S.dma('sp', lambda e, bp=bp, bb=bb, sbuf=sbuf: e.dma_start(out=sgs[2 * bp + bb].rearrange("h d e -> d h e"), in_=sbuf[:, bb, :, :]),
                                  reads=[skey])
                if samp and DEBUG:
                    dve(lambda e: e.tensor_copy(out=DB2[:, :], in_=PS[0][:, 0:512]), [PK(0)], ['TO'])
                    dbg("ot_full", DB2[:, :], [128, 512], ['TO'])
                for bk in range(len(obanks)):
                    w = 4 * C if C == 128 else 8 * C
                    nq = 4 if C == 128 else 8
                    act(SQO[:, bk * 4:bk * 4 + nq, :], PS[bk][:, 0:w].rearrange("p (a c) -> p a c", a=nq), AF.Square, [obanks[bk]], ['SQO'])
                mms = []
                for h in range(4):
                    for ec in range(2):
                        mms.append((PS[6][:, h * C:(h + 1) * C], O256, SQO[:, h * 2 + ec, :], ec == 0, ec == 1))
                pe_group(mms, ['SQO'], [PK(6)])
                rstd_from_ps(PS[6][:, 0:4 * C], 4 * C, RSO[:].rearrange("p h c -> p (h c)"), [PK(6)], ['RSO'])
                for bk in range(len(obanks)):
                    nh = 2 if C == 128 else 4
                    w = 4 * C if C == 128 else 8 * C
                    dve(lambda e, bk=bk, nh=nh, w=w: e.tensor_tensor(
                        out=TO[:, bk * 4:bk * 4 + 2 * nh, :].rearrange("p (h x) c -> p h x c", x=2),
                        in0=PS[bk][:, 0:w].rearrange("p (h x c) -> p h x c", h=nh, x=2),
                        in1=RSO[:, bk * 2:bk * 2 + nh, :].unsqueeze(2).to_broadcast([128, nh, 2, C]), op=ALU.mult),
                        [obanks[bk], 'RSO'], ['TO'])
                if samp:
                    dbg("gla_to", TO[:, :, :], [128, 8, 64], ['TO'])
                    dbg("gla_sr0", SR[:, :, :], [128, 8, 64], ['SR']) if False else None
                dve(lambda e, bs=bs: e.tensor_tensor(out=SR[:, :, bs], in0=TO[:], in1=SR[:, :, bs], op=ALU.mult), ['TO', 'SR'], ['SR'])
            for half in range(2):
                wv, wk = wslab(w_out, half * 512, 512)
                for mm_ in range(4):
                    m = half * 4 + mm_
                    p = 4 + m % 2
                    pe_group([(PS[p][:, 0:n], wv[:, c8, mm_ * 128:(mm_ + 1) * 128], SR[:, c8, :], c8 == 0, c8 == KC - 1) for c8 in range(KC)],
                             [wk, 'SR'], [PK(p)])
                    resid_add(t, 0, 1, m, PS[p][:, 0:n], PK(p))
            if bar:
                S.barrier()

        KNS = sb("KNS", [128, KC, NS], BF16)
        VNS = sb("VNS", [NS, D], BF16)

        def kv_tile(t, koff, emit_out, bar=True):
            c0, n, samp, idx = t
            C = 64 if samp else 128
            nblk = n // C
            ra = RA()
            KF = [ra.get([128, n], F32) for _ in range(2)]
            SQK = ra.get([128, n], BF16); RSK = ra.get([128, n], F32)
            KN = [ra.get([128, n], F32) for _ in range(2)]
            KNB = [ra.get([128, n], BF16) for _ in range(2)]
            VF = ra.get([C, nblk, 1024], F32)
            VA = [ra.get([C, 16, 128], BF16) for _ in range(2)]
            LX = ra.get([C, 16], F32)
            ukey = ('U', idx)
            norm_mod(t, 'kv', None, 0, ukey)
            for i in range(2):
                dve(lambda e, i=i: e.memset(VA[i][:, :, 64:128], 1.0), [], [('VA', i)])
            for half in range(2):
                wv, wk = wslab(w_kvf, half * 512, 512)
                for cc in range(4):
                    c8 = half * 4 + cc
                    p = c8 % 2
                    r = c8 % 2
                    pe_group([(PS[p][:, 0:n], wv[:, k, cc * 128:(cc + 1) * 128], U[:, k, 0:n], k == 0, k == KC - 1) for k in range(KC)],
                             [wk, ukey], [PK(p)])
                    act(KF[r][:, :], PS[p][:, 0:n], AF.Copy, [PK(p)], [('KF', r)])
                    act(SQK[:, :], KF[r][:, :], AF.Square, [('KF', r)], ['SQK'])
                    pe_group([(PS[2][:, 0:n], OBLKB, SQK[:, :], True, True)], ['SQK'], [PK(2)])
                    rstd_from_ps(PS[2][:, 0:n], n, RSK[:, :], [PK(2)], ['RSK'])
                    dve(lambda e, r=r: e.scalar_tensor_tensor(out=KN[r][:, :], in0=KF[r][:, :], scalar=GK[:, 0:1], in1=RSK[:, :], op0=ALU.mult, op1=ALU.mult),
                        [('KF', r), 'RSK', 'small'], [('KN', r)])
                    if samp:
                        S.dma('sp', lambda e, r=r, c8=c8: e.dma_start(out=kso[c8 * 128:(c8 + 1) * 128, :], in_=KN[r][:, :]), reads=[('KN', r)])
                        dve(lambda e, r=r, c8=c8: e.tensor_copy(out=KNS[:, c8, :], in_=KN[r][:, :]), [('KN', r)], ['KNS'])
                    else:
                        if emit_out:
                            S.dma('sp', lambda e, r=r, c8=c8: e.dma_start(out=ko[c8 * 128:(c8 + 1) * 128, c0:c0 + n], in_=KN[r][:, :]), reads=[('KN', r)])
                        act(KNB[r][:, :], KN[r][:, :], AF.Copy, [('KN', r)], [('KNB', r)])
                        for hh in range(2):
                            S.dma('sp', lambda e, r=r, c8=c8, hh=hh: e.dma_start(out=KTD[2 * c8 + hh, 0:64, koff + c0:koff + c0 + n],
                                                                                 in_=KNB[r][hh * 64:(hh + 1) * 64, :]),
                                  reads=[('KNB', r)], writes=['KTD'])
            for half in range(2):
                wv, wk = wslab(w_kvf, 1024 + half * 512, 512)
                for b in range(nblk):
                    p = 3 + b % 2
                    pe_group([(PS[p][0:C, 0:512], U[:, k, b * C:(b + 1) * C], wv[:, k, 0:512], k == 0, k == KC - 1) for k in range(KC)],
                             [wk, ukey], [PK(p)])
                    act(VF[:, b, half * 512:(half + 1) * 512], PS[p][0:C, 0:512], AF.Copy, [PK(p)], [('VF', b)])
            wv, wk = wslab(w_kvf, 2048, 16)
            for b in range(nblk):
                blk = 32 if samp else (koff + c0) // 128 + b
                r = b % 2
                if samp:
                    S.dma('sp', lambda e: e.dma_start(out=vso[:, :], in_=VF[:, 0, :]), reads=[('VF', 0)])
                    dve(lambda e: e.tensor_copy(out=VNS[:, :], in_=VF[:, 0, :]), [('VF', 0)], ['VNS'])
                else:
                    if emit_out:
                        S.dma('sp', lambda e, b=b: e.dma_start(out=vo[c0 + b * 128:c0 + (b + 1) * 128, :], in_=VF[:, b, :]), reads=[('VF', b)])
                    dve(lambda e, b=b, r=r: e.tensor_copy(out=VA[r][:, :, 0:64], in_=VF[:, b, :].rearrange("p (h x) -> p h x", h=16)),
                        [('VF', b)], [('VA', r)])
                    S.dma('sp', lambda e, b=b, r=r: e.dma_start(out=VD[koff + c0 + b * 128:koff + c0 + (b + 1) * 128, :, :], in_=VA[r][:, :, :]),
                          reads=[('VA', r)], writes=['VD'])
                pe_group([(PS[5][0:C, 0:16], U[:, k, b * C:(b + 1) * C], wv[:, k, 0:16], k == 0, k == KC - 1) for k in range(KC)], [wk, ukey], [PK(5)])
                dve(lambda e: e.tensor_tensor(out=LX[:, :], in0=PS[5][0:C, 0:16], in1=BFB[0:C, :], op=ALU.add), [PK(5), 'small'], ['LX'])
                act(LX[:, :], LX[:, :], AF.Exp, ['LX'], ['LX'], scale=-1.0)
                act(LX[:, :], LX[:, :], AF.Ln, ['LX'], ['LX'], bias=EPSC[0:C, 1:2], scale=1.0)
                dve(lambda e, blk=blk: e.tensor_scalar(out=LFT[0:C, blk, :], in0=LX[:, :], scalar1=-1.0, scalar2=None, op0=ALU.mult), ['LX'], ['LFT'])
                if samp:
                    S.dma('sp', lambda e, blk=blk: e.dma_start(out=lfso[:, :], in_=LFT[0:C, blk, :]), reads=['LFT'])
                elif emit_out:
                    S.dma('sp', lambda e, b=b, blk=blk: e.dma_start(out=lfo[c0 + b * 128:c0 + (b + 1) * 128, :], in_=LFT[:, blk, :]), reads=['LFT'])
            if bar:
                S.barrier()

        def f_pipeline():
            ra = RA()
            LFE = ra.get([128, 32, 16], F32); TOTS = ra.get([128, 32, 16], F32); CAR = ra.get([128, 32, 16], F32)
            FTM = ra.get([128, 32, 16], F32); NFK = ra.get([128, 32, 16], F32)
            FTX = ra.get([16, 2048], F32); R1 = ra.get([16, 2048], F32)
            SPL = ra.get([16, 3, 2048], BF16); ONE3 = ra.get([16, 3, 2048], BF16)
            dve(lambda e: e.tensor_scalar(out=LFE[:, 0:16, :], in0=LFT[:, 0:16, :], scalar1=FLG[:, 0:1], scalar2=None, op0=ALU.mult), ['LFT', 'small'], ['LFE'])
            dve(lambda e: e.tensor_copy(out=LFE[:, 16:32, :], in_=LFT[:, 16:32, :]), ['LFT'], ['LFE'])
            dve(lambda e: e.memset(ONE3[:], 1.0), [], ['ONE3'])
            lfe = LFE[:].rearrange("p b h -> p (b h)")
            pe_group([(PS[0][:, 0:512], cf(C_M01), lfe, True, True)], ['LFE', 'CF'], [PK(0)])
            pe_group([(PS[1][:, 0:512], cf(C_ONES), lfe, True, True)], ['LFE', 'CF'], [PK(1)])
            act(TOTS[:].rearrange("p b h -> p (b h)"), PS[1][:, 0:512], AF.Copy, [PK(1)], ['TOTS'])
            dve(lambda e: e.memset(CAR[:, 0, :], 0.0), [], ['CAR'])
            for b in range(1, 32):
                dve(lambda e, b=b: e.tensor_tensor(out=CAR[:, b, :], in0=CAR[:, b - 1, :], in1=TOTS[:, b - 1, :], op=ALU.add), ['CAR', 'TOTS'], ['CAR'])
            dve(lambda e: e.tensor_tensor(out=FTM[:].rearrange("p b h -> p (b h)"), in0=PS[0][:, 0:512], in1=CAR[:].rearrange("p b h -> p (b h)"), op=ALU.add),
                [PK(0), 'CAR'], ['FTM'])
            dve(lambda e: e.tensor_scalar(out=NFK[:, 0:16, :], in0=FTM[:, 0:16, :], scalar1=FLG[:, 1:2], scalar2=-1.0, op0=ALU.add, op1=ALU.mult), ['FTM', 'small'], ['NFK'])
            dve(lambda e: e.tensor_scalar(out=NFK[:, 16:32, :], in0=FTM[:, 16:32, :], scalar1=-1.0, scalar2=None, op0=ALU.mult), ['FTM'], ['NFK'])

            def split_and_store(src, blk0, dsts):
                for g in range(4):
                    pe_group([(PS[2 + g % 2][0:16, j * 128:(j + 1) * 128], src[:, blk0 + g * 4 + j, :], cf(C_IDENT), True, True) for j in range(4)],
                             ['FTM', 'NFK', 'CF'], [PK(2 + g % 2)])
                    act(FTX[:, g * 512:(g + 1) * 512], PS[2 + g % 2][0:16, 0:512], AF.Copy, [PK(2 + g % 2)], ['FTX'])
                act(SPL[:, 0, :], FTX[:, :], AF.Copy, ['FTX'], ['SPL'])
                dve(lambda e: e.tensor_tensor(out=R1[:, :], in0=FTX[:, :], in1=SPL[:, 0, :], op=ALU.subtract), ['FTX', 'SPL'], ['R1'])
                act(SPL[:, 1, :], R1[:, :], AF.Copy, ['R1'], ['SPL'])
                dve(lambda e: e.tensor_tensor(out=R1[:, :], in0=R1[:, :], in1=SPL[:, 1, :], op=ALU.subtract), ['R1', 'SPL'], ['R1'])
                act(SPL[:, 2, :], R1[:, :], AF.Copy, ['R1'], ['SPL'])
                (d_spl, d_one, dkey) = dsts
                S.dma('sp', lambda e: e.dma_start(out=d_spl, in_=SPL[:, :, :]), reads=['SPL'], writes=[dkey])
                S.dma('sp', lambda e: e.dma_start(out=d_one, in_=ONE3[:, :, :]), reads=['ONE3'], writes=[dkey])

            split_and_store(NFK, 0, (KTD[:, 67:70, 0:2048], KTD[:, 64:67, 0:2048], 'KTD'))
            split_and_store(NFK, 16, (KTD[:, 67:70, 2048:4096], KTD[:, 64:67, 2048:4096], 'KTD'))
            split_and_store(FTM, 16, (QD[:, 64:67, 0:2048], QD[:, 67:70, 0:2048], 'QD'))
            dbg("ftm", FTM[:, :, :], [128, 32, 16], ['FTM'])
            dbg("nfk", NFK[:, :, :], [128, 32, 16], ['NFK'])
            dbg("ftx", FTX[:, :], [16, 2048], ['FTX'])
            dbg("r1", R1[:, :], [16, 2048], ['R1'])
            SPF = CAR[:].rearrange("p b h -> p (b h)")[0:16, :]
            for i3 in range(3):
                dve(lambda e, i3=i3: e.tensor_copy(out=SPF, in_=SPL[:, i3, 0:512]), ['SPL', 'SPF'], ['SPF'])
                dbg("spf%d" % i3, SPF, [16, 512], ['SPF'])
            S.barrier()

        def qg_proj(t, ra, qsink, SG):
            c0, n, samp, idx = t
            QF = [ra.get([128, n], F32) for _ in range(2)]
            SQ = ra.get([128, n], BF16); RSQ = ra.get([128, n], F32)
            QNB = [ra.get([128, n], BF16) for _ in range(2)]
            ukey = ('U', idx)
            norm_mod(t, 1, 1, 0, ukey)
            for half in range(2):
                wv, wk = wslab(w_qg, half * 512, 512)
                for cc in range(4):
                    c8 = half * 4 + cc
                    p = c8 % 2
                    r = c8 % 2
                    pe_group([(PS[p][:, 0:n], wv[:, k, cc * 128:(cc + 1) * 128], U[:, k, 0:n], k == 0, k == KC - 1) for k in range(KC)],
                             [wk, ukey], [PK(p)])
                    act(QF[r][:, :], PS[p][:, 0:n], AF.Copy, [PK(p)], [('QF', r)])
                    act(SQ[:, :], QF[r][:, :], AF.Square, [('QF', r)], ['SQ'])
                    pe_group([(PS[2][:, 0:n], OBLKB, SQ[:, :], True, True)], ['SQ'], [PK(2)])
                    rstd_from_ps(PS[2][:, 0:n], n, RSQ[:, :], [PK(2)], ['RSQ'])
                    dve(lambda e, r=r: e.scalar_tensor_tensor(out=QNB[r][:, :], in0=QF[r][:, :], scalar=GQ[:, 0:1], in1=RSQ[:, :], op0=ALU.mult, op1=ALU.mult),
                        [('QF', r), 'RSQ', 'small'], [('QNB', r)])
                    qsink(c8, QNB[r], ('QNB', r))
            for half in range(2):
                wv, wk = wslab(w_qg, 1024 + half * 512, 512)
                for cc in range(4):
                    c8 = half * 4 + cc
                    p = 3 + c8 % 2
                    pe_group([(PS[p][:, 0:n], wv[:, k, cc * 128:(cc + 1) * 128], U[:, k, 0:n], k == 0, k == KC - 1) for k in range(KC)],
                             [wk, ukey], [PK(p)])
                    act(SG[:, c8, :], PS[p][:, 0:n], AF.Sigmoid, [PK(p)], ['SG'])

        def wo_proj(t, OG):
            c0, n, samp, idx = t
            for half in range(2):
                wv, wk = wslab(w_o, half * 512, 512)
                for mm_ in range(4):
                    m = half * 4 + mm_
                    p = 6 + m % 2
                    pe_group([(PS[p][:, 0:n], wv[:, c8, mm_ * 128:(mm_ + 1) * 128], OG[:, c8, :], c8 == 0, c8 == KC - 1) for c8 in range(KC)],
                             [wk, 'OG'], [PK(p)])
                    resid_add(t, 1, 1, m, PS[p][:, 0:n], PK(p))

        def fox_prompt_tile(t):
            c0, n, samp, qt = t
            ra = RA()
            SG = ra.get([128, 8, 512], BF16)
            base = ra.off
            ra2 = RA(); ra2.off = base

            def qsink(c8, qnb, key):
                for hh in range(2):
                    S.dma('sp', lambda e, hh=hh: e.dma_start(out=QD[2 * c8 + hh, 0:64, c0:c0 + n], in_=qnb[hh * 64:(hh + 1) * 64, :]),
                          reads=[key], writes=['QD'])
            qg_proj(t, ra2, qsink, SG)
            S.barrier()
            ra.off = base
            nkeys = NPR + (qt + 1) * 512
            nkb = nkeys // 128
            KHB = [ra.get([70, 4096], BF16) for _ in range(2)]
            VHB = [ra.get([128, 32, 128], BF16) for _ in range(2)]
            QHB = [ra.get([70, 512], BF16) for _ in range(2)]
            PTB = [ra.get([128, 512], BF16) for _ in range(4)]
            OS = ra.get([128, 512], F32); RD = ra.get([64, 512], F32); ONB = ra.get([64, 512], BF16)
            for h in range(16):
                r = h % 2
                kk, vk, qk = ('KHB', r), ('VHB', r), ('QHB', r)
                S.dma('pool', lambda e, h=h, r=r: e.dma_start(out=KHB[r][:, 0:nkeys], in_=KTD[h, :, 0:nkeys]), reads=['KTD'], writes=[kk])
                S.dma('pool', lambda e, h=h, r=r: e.dma_start(out=VHB[r][:, 0:nkb, :], in_=VD[0:nkeys, h, :].rearrange("(b p) x -> p b x", p=128)),
                      reads=['VD'], writes=[vk])
                S.dma('pool', lambda e, h=h, r=r: e.dma_start(out=QHB[r][:, :], in_=QD[h, :, c0:c0 + n]), reads=['QD'], writes=[qk])
                oa = 4 + h % 2

                def emit_st(kb, h=h, r=r, kk=kk, qk=qk):
                    diag = kb - (16 + 4 * qt)
                    col0 = max(0, diag) * 128
                    sp_ = kb % 4
                    mms = [(PS[sp_][:, col0:512], KHB[r][:, kb * 128:(kb + 1) * 128], QHB[r][:, col0:512], True, diag < 0)]
                    if diag >= 0:
                        mms.append((PS[sp_][:, col0:col0 + 128], IDB, NEGMB, False, True))
                    pe_group(mms, [kk, qk, 'CB'], [PK(sp_)])
                    act(PTB[sp_][:, col0:512], PS[sp_][:, col0:512], AF.Exp, [PK(sp_)], [('PTB', sp_)])

                def emit_pv(kb, h=h, r=r, vk=vk, oa=oa):
                    diag = kb - (16 + 4 * qt)
                    col0 = max(0, diag) * 128
                    sp_ = kb % 4
                    pe_group([(PS[oa][:, col0:512], VHB[r][:, kb, :], PTB[sp_][:, col0:512], kb == 0, kb == nkb - 1)], [vk, ('PTB', sp_)], [PK(oa)])
                LOOK = 2
                for kb in range(nkb + LOOK):
                    if kb < nkb:
                        emit_st(kb)
                    if kb >= LOOK:
                        emit_pv(kb - LOOK)
                act(OS[:, :], PS[oa][:, 0:512], AF.Copy, [PK(oa)], ['OS'])
                if qt == 0 and h == 0:
                    dbg("os", OS[:, :], [128, 512], ['OS'])
                pe_group([(PS[6][0:64, 0:512], cf(C_SHIFT)[:, 0:64], OS[:, :], True, True)], ['OS', 'CF'], [PK(6)])
                dve(lambda e: e.reciprocal(out=RD[:, :], in_=PS[6][0:64, 0:512]), [PK(6)], ['RD'])
                if qt == 0 and h == 0:
                    dbg("rd", RD[:, :], [64, 512], ['RD'])
                dve(lambda e: e.tensor_tensor(out=ONB[:, :], in0=OS[0:64, :], in1=RD[:, :], op=ALU.mult), ['OS', 'RD'], ['ONB'])
                S.dma('sp', lambda e, h=h: e.dma_start(out=OD[h * 64:(h + 1) * 64, c0:c0 + n], in_=ONB[:, :]), reads=['ONB'], writes=['OD'])
            S.barrier()
            ra.off = base
            OG = ra.get([128, 8, 512], BF16)
            S.dma('sp', lambda e: e.dma_start(out=OG[:, :, :], in_=OD[:, c0:c0 + n].rearrange("(c p) n -> p c n", p=128)), reads=['OD'], writes=['OG'])
            dve(lambda e: e.tensor_tensor(out=OG[:], in0=OG[:], in1=SG[:], op=ALU.mult), ['OG', 'SG'], ['OG'])
            wo_proj(t, OG)
            S.barrier()

        def fox_sample():
            t = S_TILE
            n = NS
            ra = RA()
            SGS = ra.get([128, 8, NS], BF16)
            QBD = ra.get([128, NSB, 8, 8], BF16)
            dve(lambda e: e.memset(QBD[:], 0.0), [], ['QBD'])

            def qsink(c8, qnb, key):
                for hh in range(2):
                    ps_ = slice(hh * 64, (hh + 1) * 64)
                    dve(lambda e, ps_=ps_, hh=hh: e.tensor_copy(out=QBD[ps_, :, c8, hh * 4:(hh + 1) * 4],
                                                                in_=qnb[ps_, :].rearrange("p (b t) -> p b t", t=4)), [key], ['QBD'])
            base0 = ra.off
            ra2 = RA(); ra2.off = base0
            qg_proj(t, ra2, qsink, SGS)
            S.barrier()
            ra.off = base0
            PTI = ra.get([128, NSB * NPG], I32 if False else F32).bitcast(I32)
            IOT = ra.get([128, 1], F32).bitcast(I32)
            IDX = ra.get([128, NSB * NPG], F32).bitcast(I32)
            MKN = ra.get([64, 1024], F32); BIASN = MKN; NCN = ra.get([64, 16], F32)
            LP = [ra.get([128, NPG, 16], F32) for _ in range(2)]
            RBs = [ra.get([128, NPG, 16], F32) for _ in range(2)]; TSs = [ra.get([128, NPG, 16], F32) for _ in range(2)]
            KVP = [ra.get([128, 2, 2048], BF16) for _ in range(2)]
            KTt = [ra.get([128, 8, 2, 128], BF16) for _ in range(2)]
            SBt = ra.get([128, 128], F32); PTp = [ra.get([128, 2, 64], BF16) for _ in range(3)]
            SNB = ra.get([64, 64], F32); PTN = ra.get([64, 64], BF16)
            off3 = ra.off
            KVP.append(carve(off3, [128, 2, 2048], BF16))
            KP = [b_[:, :, 0:1024] for b_ in KVP]
            VP = [b_[:, :, 1024:2048] for b_ in KVP]
            assert ra.off <= off3
            ra.off = off3
            OF = ra.get([128, 1024], F32); RDS = ra.get([128, 1024], F32)
            OGS = ra.get([128, 8, NS], BF16)
            S.dma('sp', lambda e: e.dma_start(out=PTI[:, :], in_=ptab[0:1, :].partition_broadcast(128)), writes=['PTI'])
            S.dma('sp', lambda e: e.dma_start(out=IOT[:, :], in_=iota[:, :]), writes=['IOT'])
            S.dma('sp', lambda e: e.dma_start(out=MKN[:, :], in_=maskn[:, :]), writes=['MKN'])
            dve(lambda e: e.tensor_scalar(out=IDX[:, :], in0=PTI[:, :], scalar1=128, scalar2=IOT[:, 0:1], op0=ALU.mult, op1=ALU.add), ['PTI', 'IOT'], ['IDX'])
            pe_group([(PS[3][0:64, 0:16], cf(C_TRI1S)[0:64, 0:64], LFT[0:64, 32, :], True, True)], ['LFT', 'CF'], [PK(3)])
            dve(lambda e: e.tensor_scalar(out=NCN[:, :], in0=PS[3][0:64, 0:16], scalar1=-1.0, scalar2=None, op0=ALU.mult), [PK(3)], ['NCN'])
            dve(lambda e: e.tensor_tensor(out=BIASN[:, :].rearrange("p (b h t) -> p b h t", b=16, h=16), in0=MKN[:, :].rearrange("p (b h t) -> p b h t", b=16, h=16),
                                          in1=NCN[:, :].unsqueeze(1).unsqueeze(3).to_broadcast([64, 16, 16, 4]), op=ALU.add), ['MKN', 'NCN'], ['MKN', 'BIASN'])
            def emit_R(b):
                lp = LP[b % 2]
                lk = ('LP', b % 2)
                RB = RBs[b % 2]; TS = TSs[b % 2]
                rk = ('RB', b % 2); tk_ = ('TS', b % 2)
                for pg in range(NPG):
                    i = b * NPG + pg
                    S.dma('pool', lambda e, i=i, pg=pg, lp=lp: e.indirect_dma_start(out=lp[:, pg, :], out_offset=None, in_=clf[:, :],
                                                                                    in_offset=bass.IndirectOffsetOnAxis(ap=IDX[:, i:i + 1], axis=0)),
                          reads=['IDX'], writes=[lk])
                lpf = lp[:].rearrange("p a h -> p (a h)")
                pe_group([(PS[3][:, 0:256], cf(C_USTR), lpf, True, True), (PS[3][:, 256:512], cf(C_ONES), lpf, False, True)], [lk, 'CF'], [PK(3)])
                act(RB[:].rearrange("p a h -> p (a h)"), PS[3][:, 0:256], AF.Copy, [PK(3)], [rk])
                act(TS[:].rearrange("p a h -> p (a h)"), PS[3][:, 256:512], AF.Copy, [PK(3)], [tk_])
                for p2 in range(1, NPG):
                    dve(lambda e, p2=p2, RB=RB, TS=TS: e.tensor_tensor(out=RB[:, 0:p2, :], in0=RB[:, 0:p2, :],
                                                                       in1=TS[:, p2, :].unsqueeze(1).to_broadcast([128, p2, 16]), op=ALU.add),
                        [rk, tk_], [rk])

            def stage_A(b, pp):
                gi = b * (NPG // 2) + pp
                r = gi % 3
                r2 = gi % 2
                kk, vk = ('KP', r), ('VP', r)
                RB = RBs[b % 2]
                for pg in range(2):
                    i = b * NPG + pp * 2 + pg
                    S.dma('pool', lambda e, i=i, pg=pg, r=r: e.indirect_dma_start(out=KVP[r][:, pg, :], out_offset=None, in_=ckv[:, :],
                                                                                  in_offset=bass.IndirectOffsetOnAxis(ap=IDX[:, i:i + 1], axis=0)),
                          reads=['IDX'], writes=[kk, vk])
                tk = ('KTt', r2)
                for g in range(4):
                    pb_ = g % 2
                    mms = []
                    for cc in range(2):
                        for pg in range(2):
                            c8 = 2 * g + cc
                            mms.append((PS[pb_][:, (cc * 2 + pg) * 128:(cc * 2 + pg + 1) * 128], KP[r][:, pg, c8 * 128:(c8 + 1) * 128], IDB, True, True))
                    pe_group(mms, [kk, 'CB'], [PK(pb_)])
                    if g % 2 == 0:
                        act(KTt[r2][:, 2 * g:2 * g + 2, :, :].rearrange("p a b c -> p (a b c)"), PS[pb_][:, 0:512], AF.Copy, [PK(pb_)], [tk])
                    else:
                        dve(lambda e, g=g, pb_=pb_, r=r: e.tensor_copy(out=KTt[r2][:, 2 * g:2 * g + 2, :, :].rearrange("p a b c -> p (a b c)"), in_=PS[pb_][:, 0:512]),
                            [PK(pb_)], [tk], big=True)
                mms = []
                for pg in range(2):
                    for c8 in range(8):
                        mms.append((PS[2][:, pg * 64 + c8 * 8:pg * 64 + (c8 + 1) * 8], KTt[r2][:, c8, pg, :], QBD[:, b, c8, :], True, True))
                pe_group(mms, [tk, 'QBD'], [PK(2)])
                dve(lambda e, pp=pp, RB=RB: e.tensor_tensor(out=SBt[:, :].rearrange("p (a h t) -> p a h t", a=2, h=16),
                                                            in0=PS[2][:, 0:128].rearrange("p (a h t) -> p a h t", a=2, h=16),
                                                            in1=RB[:, 2 * pp:2 * pp + 2, :].unsqueeze(3).to_broadcast([128, 2, 16, 4]), op=ALU.add),
                    [PK(2), ('RB', b % 2)], ['SBt'])
                act(PTp[r][:].rearrange("p a x -> p (a x)"), SBt[:, :], AF.Exp, ['SBt'], [('PTp', r)])

            def stage_B(b, pp):
                gi = b * (NPG // 2) + pp
                r = gi % 3
                vk, pk = ('VP', r), ('PTp', r)
                ob = 4 + b // 8
                db = 6 + b // 8
                bo = (b % 8) * 64
                mms = []
                for pg in range(2):
                    first = (pp == 0 and pg == 0)
                    for c8 in range(8):
                        mms.append((PS[ob][:, bo + c8 * 8:bo + (c8 + 1) * 8], VP[r][:, pg, c8 * 128:(c8 + 1) * 128], PTp[r][:, pg, c8 * 8:(c8 + 1) * 8],
                                    first and c8 == 0, False))
                    mms.append((PS[db][:, bo:bo + 64], ONEB, PTp[r][:, pg, :], first, False))
                pe_group(mms, [vk, pk, 'CB'], [PK(ob), PK(db)])
                if pp != NPG // 2 - 1:
                    return
                pe_group([(PS[3][0:64, c8 * 8:(c8 + 1) * 8], KNS[:, c8, :], QBD[:, b, c8, :], True, True) for c8 in range(8)], ['KNS', 'QBD'], [PK(3)])
                dve(lambda e, b=b: e.tensor_tensor(out=SNB[:, :], in0=PS[3][0:64, 0:64], in1=BIASN[:, b * 64:(b + 1) * 64], op=ALU.add), [PK(3), 'BIASN'], ['SNB'])
                act(PTN[:, :], SNB[:, :], AF.Exp, ['SNB'], ['PTN'])
                mms = []
                for c8 in range(8):
                    mms.append((PS[ob][:, bo + c8 * 8:bo + (c8 + 1) * 8], VNS[:, c8 * 128:(c8 + 1) * 128], PTN[:, c8 * 8:(c8 + 1) * 8], False, True))
                mms.append((PS[db][:, bo:bo + 64], ONEB[0:64, :], PTN[:, :], False, True))
                pe_group(mms, ['VNS', 'PTN', 'CB'], [PK(ob), PK(db)])

            items = [(b, pp) for b in range(NSB) for pp in range(NPG // 2)]
            emit_R(0)
            for i, (b, pp) in enumerate(items):
                if pp == 0 and b + 1 < NSB:
                    emit_R(b + 1)
                stage_A(b, pp)
                if i >= 1:
                    stage_B(*items[i - 1])
            stage_B(*items[-1])
            for hb in range(2):
                dve(lambda e, hb=hb: e.reciprocal(out=RDS[:, hb * 512:(hb + 1) * 512], in_=PS[6 + hb][:, 0:512]), [PK(6 + hb)], ['RDS', ('VP', 2)])
                dve(lambda e, hb=hb: e.tensor_tensor(out=OF[:, hb * 512:(hb + 1) * 512], in0=PS[4 + hb][:, 0:512], in1=RDS[:, hb * 512:(hb + 1) * 512], op=ALU.mult),
                    [PK(4 + hb), 'RDS'], ['OF', ('KP', 2)])
            for hh in range(2):
                ps_ = slice(hh * 64, (hh + 1) * 64)
                dve(lambda e, ps_=ps_, hh=hh: e.tensor_copy(out=OGS[ps_, :, :].rearrange("p c (b t) -> p b c t", t=4),
                                                            in_=OF[ps_, :].rearrange("p (b c x) -> p b c x", b=16, c=8)[:, :, :, hh * 4:(hh + 1) * 4]),
                    ['OF'], ['OG'])
            dve(lambda e: e.tensor_tensor(out=OGS[:], in0=OGS[:], in1=SGS[:], op=ALU.mult), ['OG', 'SG'], ['OG'])
            wo_proj(t, OGS)
            S.barrier()

        def load_x(src, c0, n, tiles):
            for k in range(KC):
                for t in tiles:
                    S.dma('sp', lambda e, k=k, t=t: e.dma_start(out=H[:, k, t[0]:t[0] + t[1]], in_=src[k * 128:(k + 1) * 128, t[0] - c0:t[0] - c0 + t[1]]),
                          writes=[hk(t)])

        G0 = [PT_TILES[0], PT_TILES[1]]
        G1 = [PT_TILES[2], PT_TILES[3]]
        G1S = [PT_TILES[2], PT_TILES[3], S_TILE]
        load_x(xp, 0, NPR, PT_TILES)
        dve(lambda e: e.memset(SST[:], 0.0), [], ['SST'])
        act(SBF[:], SST[:], AF.Copy, ['SST'], ['SBF'])
        ffn(0, 0, 0, G0); ffn(0, 0, 0, G1)
        for t in PT_TILES:
            gla_tile(t, bar=(t is PT_TILES[-1]))
        ffn(0, 1, 2, G0); ffn(0, 1, 2, G1)
        for t in PT_TILES:
            kv_tile(t, 0, False, bar=(t is PT_TILES[-1]))
        for h in range(4):
            dve(lambda e, h=h: e.tensor_scalar(out=SST[:, h, :], in0=SST[:, h, :], scalar1=FLG[:, 0:1], scalar2=None, op0=ALU.mult), ['SST', 'small'], ['SST'])
        act(SBF[:], SST[:], AF.Copy, ['SST'], ['SBF'])
        load_x(xo, 0, NPR, PT_TILES)
        load_x(xs, NPR, NS, [S_TILE])
        ffn(0, 0, 0, G0); ffn(0, 0, 0, G1S)
        dbg_h("l0_ffn0")
        for t in PT_TILES:
            gla_tile(t, bar=(t is PT_TILES[-1]))
        S.dma('sp', lambda e: e.dma_start(out=sgp.rearrange("h d e -> d h e"), in_=SST[:, :, :]), reads=['SST'])
        gla_tile(S_TILE)
        dbg_h("l0_mix")
        ffn(0, 1, 2, G0); ffn(0, 1, 2, G1S)
        dbg_h("l0_ffn1")
        for t in PT_TILES:
            kv_tile(t, NPR, True, bar=(t is PT_TILES[-1]))
        kv_tile(S_TILE, 0, True)
        f_pipeline()
        ffn(1, 0, 0, G0); ffn(1, 0, 0, G1S)
        dbg_h("l1_ffn0")
        for t in PT_TILES:
            fox_prompt_tile(t)
        fox_sample()
        dbg_h("l1_mix")
        ffn(1, 1, 2, G0); ffn(1, 1, 2, G1S)
        for k in range(KC):
            S.dma('sp', lambda e, k=k: e.dma_start(out=yo[k * 128:(k + 1) * 128, :], in_=H[:, k, 0:NPR]), reads=[hk(t) for t in PT_TILES])
            S.dma('sp', lambda e, k=k: e.dma_start(out=ys[k * 128:(k + 1) * 128, :], in_=H[:, k, NPR:NT]), reads=[hk(S_TILE)])
        S.finish()
        S.emit(block)
    return nc


def _shared_inputs(inp):
    f = lambda a: np.ascontiguousarray(np.asarray(a, dtype=np.float32))
    b_ada = np.asarray(inp['b_ada'], np.float32)
    g_norm = np.asarray(inp['g_norm'], np.float32)
    sh = {
        'w_ada': f(inp['w_ada']),
        'b_adaT': f(np.stack([b_ada[l].reshape(72, 128).T for l in range(2)], axis=1)),
        'g_normT': f(np.stack([np.stack([g_norm[l, s].reshape(8, 128).T for s in range(3)], axis=1) for l in range(2)], axis=1)),
        'w_up': f(inp['w_ffn_up']), 'w_dn': f(inp['w_ffn_down']),
        'w_in': f(np.asarray(inp['gla_w_in'])[0]),
        'wg2b': f(np.concatenate([np.asarray(inp['gla_w_gate2'])[0], np.asarray(inp['gla_b_gate'])[0][None, :]], axis=0)),
        'goutT': f(np.asarray(inp['gla_g_out'])[0].reshape(2, 128).T),
        'w_out': f(np.asarray(inp['gla_w_out'])[0]),
        'w_akv': f(inp['w_ada_kv']),
        'b_akvT': f(np.asarray(inp['b_ada_kv']).reshape(16, 128).T),
        'g_kvT': f(np.asarray(inp['g_kv']).reshape(8, 128).T),
        'w_kvf': f(inp['w_kvf']),
        'bfb': f(np.tile(np.asarray(inp['b_f'])[None, :], (128, 1))),
        'gk2': f(np.tile(np.asarray(inp['g_k']), 2)[:, None]),
        'w_qg': f(np.asarray(inp['fox_w_qg'])[0]),
        'gq2': f(np.tile(np.asarray(inp['fox_g_q'])[0], 2)[:, None]),
        'w_o': f(np.asarray(inp['fox_w_o'])[0]),
        'cpack': _consts(), 'maskn': _maskn(),
        'iota': np.arange(128, dtype=np.int32)[:, None],
    }
    return sh


def _core_inputs(inp, c, shared, ckv, clf, page_table):
    f = lambda a: np.ascontiguousarray(np.asarray(a, dtype=np.float32))
    b, half = c // 2, c % 2
    xpr = np.asarray(inp['x_prompt'])
    d = dict(shared)
    d['xo'] = f(xpr[b, half * NPR:(half + 1) * NPR].T)
    d['xp'] = f(xpr[b, 0:NPR].T) if half == 1 else np.zeros((D, NPR), np.float32)
    d['xs'] = f(np.asarray(inp['x_sample'])[NSB * c:NSB * (c + 1)].reshape(NS, D).T)
    d['cT'] = f(np.concatenate([np.asarray(inp['c_prompt'])[b][None, :], np.asarray(inp['c_sample'])[NSB * c:NSB * (c + 1)]], axis=0).T)
    d['st0'] = f(np.asarray(inp['state_gla'])[0, NSB * c:NSB * (c + 1)])
    d['ckv'] = ckv
    d['clf'] = clf
    d['ptab'] = np.ascontiguousarray(np.asarray(page_table)[NSB * c:NSB * (c + 1)].reshape(1, NSB * NPG).astype(np.int32))
    d['flg'] = np.ascontiguousarray(np.tile(np.array([[float(half), BIG * (1.0 - half)]], np.float32), (128, 1)))
    return d


def _assemble(results, cores, nb_prompt, nb_sample):
    y_p = np.zeros((nb_prompt, 2 * NPR, D), np.float32); y_s = np.zeros((nb_sample, 4, D), np.float32)
    sg_p = np.zeros((1, nb_prompt, 4, 128, 256), np.float32)
    k_p = np.zeros((nb_prompt, 2 * NPR, 16, 64), np.float32); v_p = np.zeros_like(k_p)
    lf_p = np.zeros((nb_prompt, 2 * NPR, 16), np.float32)
    sg_s = np.zeros((1, nb_sample, 4, 128, 256), np.float32)
    k_s = np.zeros((nb_sample, 4, 16, 64), np.float32); v_s = np.zeros_like(k_s)
    lf_s = np.zeros((nb_sample, 4, 16), np.float32)
    for c, r in zip(cores, results):
        b, half = c // 2, c % 2
        sl = slice(half * NPR, (half + 1) * NPR)
        ss = slice(NSB * c, NSB * (c + 1))
        y_p[b, sl] = r['yo'].T
        y_s[ss] = r['ys'].T.reshape(NSB, 4, D)
        if half == 1:
            sg_p[0, b] = r['sgp']
        k_p[b, sl] = r['ko'].T.reshape(NPR, 16, 64)
        v_p[b, sl] = r['vo'].reshape(NPR, 16, 64)
        lf_p[b, sl] = r['lfo']
        sg_s[0, ss] = r['sgs']
        k_s[ss] = r['kso'].T.reshape(NSB, 4, 16, 64)
        v_s[ss] = r['vso'].reshape(NSB, 4, 16, 64)
        lf_s[ss] = r['lfso'].reshape(NSB, 4, 16)
    return (y_p, y_s, sg_p, k_p, v_p, lf_p, sg_s, k_s, v_s, lf_s)


def kernel(**inputs):
    cache_k = np.asarray(inputs['cache_k'], np.float32)
    n_phys = cache_k.shape[0]
    ckv = np.concatenate([cache_k.reshape(n_phys * 128, D), np.asarray(inputs['cache_v'], np.float32).reshape(n_phys * 128, D)], axis=1)
    clf = np.ascontiguousarray(np.asarray(inputs['cache_logf'], np.float32).reshape(n_phys * 128, 16))
    shared = _shared_inputs(inputs)
    cores = list(range(N_CORES))
    in_maps = [_core_inputs(inputs, c, shared, ckv, clf, inputs['page_table']) for c in cores]
    nc = build_program(n_phys)
    res = run_bass_kernel_spmd(nc, in_maps, core_ids=cores)
    return _assemble(res.results, cores, 4, 128)
```

```python
import contextlib
import numpy as np
import concourse.bass as bass
import concourse.mybir as mybir
from concourse.bass_utils import run_bass_kernel_spmd

F32 = mybir.dt.float32
BF16 = mybir.dt.bfloat16
I32 = mybir.dt.int32
AF = mybir.ActivationFunctionType
ALU = mybir.AluOpType
AX = mybir.AxisListType

D = 1024
KC = 8
DFF = 2816
JC = 22
NPR = 2048
NSB = 16
NS = 64
NT = NPR + NS
NPG = 16
EPS = 1e-6
BIG = 30000.0
N_CORES = 8
DEBUG = False
NOW = ''

C_IDENT, C_TRIN, C_TRUN, C_M01, C_NEGM, C_OBLK, C_ONES, C_TRINS, C_TRUNS, C_M01S, C_SHIFT, C_USTR, C_TRI1S, C_MB = range(14)
NCONST = 14


class Sched:
    ENG = ['pe', 'act', 'dve', 'pool', 'sp']

    def __init__(self, sems, dma_sems):
        self.sem = sems
        self.dsem = dma_sems
        self.prog = {e: [] for e in self.ENG}
        self.cnt = {e: 0 for e in self.ENG}
        self.duse = {q: [0] * len(v) for q, v in dma_sems.items()}
        self.dn = {q: 0 for q in dma_sems}
        self.waited = {e: {} for e in self.ENG}
        self.last_w = {}
        self.readers = {}

    def _need(self, eng, toks):
        need = {}
        for (sid, semh, val, owner, big) in toks:
            if owner == eng and (eng == 'pe' or big):
                continue
            if self.waited[eng].get(sid, 0) >= val:
                continue
            if need.get(sid, (None, 0))[1] < val:
                need[sid] = (semh, val)
        for sid, (semh, val) in need.items():
            self.waited[eng][sid] = val
        return list(need.values())

    def _deps(self, eng, reads, writes):
        toks = []
        for k in reads:
            toks.extend(self.last_w.get(k, {}).values())
        for k in writes:
            toks.extend(self.last_w.get(k, {}).values())
            toks.extend(self.readers.get(k, ()))
        return self._need(eng, toks)

    def _commit(self, tok, reads, writes):
        for k in reads:
            self.readers.setdefault(k, []).append(tok)
        for k in writes:
            self.last_w.setdefault(k, {})[tok[0]] = tok
            self.readers[k] = []

    def op(self, eng, fn, reads=(), writes=(), big=False):
        waits = self._deps(eng, reads, writes)
        self.cnt[eng] += 1
        semh = self.sem[eng]

        def run(e, waits=waits, fn=fn, semh=semh):
            for (s, v) in waits:
                e.wait_ge(s, v)
            fn(e).then_inc(semh, 1)
        self.prog[eng].append(run)
        self._commit(('E' + eng, semh, self.cnt[eng], eng, big), reads, writes)

    def dma(self, q, fn, reads=(), writes=()):
        waits = self._deps(q, reads, writes)
        slot = self.dn[q] % len(self.dsem[q])
        self.dn[q] += 1
        semh = self.dsem[q][slot]
        prev = self.duse[q][slot]
        sid = 'D%s%d' % (q, slot)
        if prev > 0 and self.waited[q].get(sid, 0) < 16 * prev:
            waits = waits + [(semh, 16 * prev)]
            self.waited[q][sid] = 16 * prev
        self.duse[q][slot] = prev + 1

        def run(e, waits=waits, fn=fn, semh=semh):
            for (s, v) in waits:
                e.wait_ge(s, v)
            fn(e).then_inc(semh, 16)
        self.prog[q].append(run)
        self._commit((sid, semh, 16 * (prev + 1), 'dma', False), reads, writes)

    def _all_tokens(self):
        toks = []
        for q, lst in self.dsem.items():
            for i, s in enumerate(lst):
                if self.duse[q][i] > 0:
                    toks.append(('D%s%d' % (q, i), s, 16 * self.duse[q][i], 'dma', False))
        for e in self.ENG:
            if self.cnt[e] > 0:
                toks.append(('E' + e, self.sem[e], self.cnt[e], e, True))
        return toks

    def barrier(self):
        toks = self._all_tokens()
        for e in self.ENG:
            waits = self._need(e, toks)
            if waits:
                def run(eng, waits=waits):
                    for (s, v) in waits:
                        eng.wait_ge(s, v)
                self.prog[e].append(run)

    def finish(self):
        waits = self._need('sp', self._all_tokens())

        def run(eng, waits=waits):
            for (s, v) in waits:
                eng.wait_ge(s, v)
        self.prog['sp'].append(run)

    def emit(self, block):
        prog = self.prog

        @block.tensor
        def _(e):
            for f in prog['pe']:
                f(e)

        @block.scalar
        def _(e):
            for f in prog['act']:
                f(e)

        @block.vector
        def _(e):
            for f in prog['dve']:
                f(e)

        @block.gpsimd
        def _(e):
            for f in prog['pool']:
                f(e)

        @block.sync
        def _(e):
            for f in prog['sp']:
                f(e)


def _consts():
    c = np.zeros((128, NCONST, 128), np.float32)
    j = np.arange(128)[:, None]
    i = np.arange(128)[None, :]
    c[:, C_IDENT] = (j == i)
    c[:, C_TRIN] = (j <= i) * (-1.0 / 16.0)
    c[:, C_TRUN] = (j > i) * (-1.0 / 16.0)
    c[:, C_M01] = (j <= i)
    c[:, C_NEGM] = (j > i) * (-BIG)
    c[:, C_OBLK] = ((j // 64) == (i // 64)) * (1.0 / 64.0)
    c[:, C_ONES] = 1.0
    same = ((j // 4) == (i // 4)) & (j < 64) & (i < 64)
    c[:, C_TRINS] = (same & (j <= i)) * (-1.0 / 16.0)
    c[:, C_TRUNS] = (same & (j > i)) * (-1.0 / 16.0)
    c[:, C_M01S] = (same & (j <= i))
    c[:, C_SHIFT] = ((j == i + 64) & (i < 64))
    c[:, C_USTR] = (j > i)
    c[:, C_TRI1S] = (same & (j <= i))
    c[:, C_MB] = ((j // 4) == i) & (j < 64) & (i < 16)
    return c


def _maskn():
    kb = (np.arange(64) // 4)[:, None, None, None]
    ks = (np.arange(64) % 4)[:, None, None, None]
    b = np.arange(16)[None, :, None, None]
    t = np.arange(4)[None, None, None, :]
    ok = (kb == b) & (ks <= t)
    m = np.where(ok, 0.0, -BIG).astype(np.float32)
    return np.ascontiguousarray(np.broadcast_to(m, (64, 16, 16, 4))).reshape(64, 1024)


def build_program(n_phys):
    nc = bass.Bass("TRN2", target_bir_lowering=False)
    NROWS = n_phys * 128

    def din(name, shape, dt=F32):
        return nc.dram_tensor(name, list(shape), dt, kind="ExternalInput").ap()

    def dout(name, shape, dt=F32):
        return nc.dram_tensor(name, list(shape), dt, kind="ExternalOutput").ap()

    def dscr(name, shape, dt):
        return nc.dram_tensor(name, list(shape), dt, kind="Internal").ap()

    xo = din("xo", [D, NPR]); xp = din("xp", [D, NPR]); xs = din("xs", [D, NS])
    cT = din("cT", [D, 17])
    st0 = din("st0", [NSB, 4, 128, 256])
    ckv = din("ckv", [NROWS, 2 * D]); clf = din("clf", [NROWS, 16])
    ptab = din("ptab", [1, NSB * NPG], I32)
    iota = din("iota", [128, 1], I32)
    flg = din("flg", [128, 2])
    w_ada = din("w_ada", [2, D, 9 * D]); b_adaT = din("b_adaT", [128, 2, 72])
    g_normT = din("g_normT", [128, 2, 3, 8])
    w_up = din("w_up", [2, 2, D, 2 * DFF]); w_dn = din("w_dn", [2, 2, DFF, D])
    w_in = din("w_in", [D, 3088]); wg2b = din("wg2b", [17, 512]); goutT = din("goutT", [128, 2])
    w_out = din("w_out", [D, D])
    w_akv = din("w_akv", [D, 2 * D]); b_akvT = din("b_akvT", [128, 16]); g_kvT = din("g_kvT", [128, 8])
    w_kvf = din("w_kvf", [D, 2064]); bfb = din("bfb", [128, 16]); gk2 = din("gk2", [128, 1])
    w_qg = din("w_qg", [D, 2 * D]); gq2 = din("gq2", [128, 1]); w_o = din("w_o", [D, D])
    cpack = din("cpack", [128, NCONST, 128]); maskn = din("maskn", [64, 1024])

    yo = dout("yo", [D, NPR]); ys = dout("ys", [D, NS])
    sgp = dout("sgp", [4, 128, 256])
    ko = dout("ko", [D, NPR]); vo = dout("vo", [NPR, D]); lfo = dout("lfo", [NPR, 16])
    sgs = dout("sgs", [NSB, 4, 128, 256])
    kso = dout("kso", [D, NS]); vso = dout("vso", [NS, D]); lfso = dout("lfso", [NS, 16])

    KTD = dscr("KTD", [16, 70, 2 * NPR], BF16)
    VD = dscr("VD", [2 * NPR, 16, 128], BF16)
    QD = dscr("QD", [16, 70, NPR], BF16)
    OD = dscr("OD", [D, NPR], BF16)

    with contextlib.ExitStack() as st:
        E = st.enter_context

        def sb(name, shape, dt=F32):
            return E(nc.sbuf_tensor(name, list(shape), dt))

        H = sb("H", [128, KC, NT])
        U = sb("U", [128, KC, 1088], BF16)
        WB = [sb("WB%d" % i, [128, 4096], BF16) for i in range(4)]
        MOD = sb("MOD", [128, 2, 72, 17])
        MODKV = sb("MODKV", [128, 16, 17])
        CF = sb("CF", [128, NCONST, 128])
        CB = sb("CB", [128, 6, 128], BF16)
        TMP = sb("TMP", [128, 2, 512])
        RSTD = sb("RSTD", [128, 512])
        SST = sb("SST", [128, 4, 256])
        SBF = sb("SBF", [128, 4, 256], BF16)
        LFT = sb("LFT", [128, 33, 16])
        BADA = sb("BADA", [128, 2, 72]); GN = sb("GN", [128, 2, 3, 8]); BAKV = sb("BAKV", [128, 16]); GKV = sb("GKV", [128, 8])
        GOUT = sb("GOUT", [128, 2]); BFB = sb("BFB", [128, 16]); GK = sb("GK", [128, 1]); GQ = sb("GQ", [128, 1])
        FLG = sb("FLG", [128, 2]); EPSC = sb("EPSC", [128, 2]); WG2 = sb("WG2", [17, 512])
        CTS = sb("CTS", [128, KC, 17]); SCT = sb("SCT", [128, KC, 17], BF16)
        PS = [E(nc.psum_tensor("PS%d" % i, [128, 512], F32)) for i in range(8)]
        RBYTES = 54000
        REG = sb("REG", [128, RBYTES // 4], F32)

        def carve(off, shape, dt):
            assert off % 4 == 0
            n = int(np.prod(shape[1:]))
            nb = n * (4 if dt == F32 else 2)
            assert off + nb <= RBYTES, (off, shape)
            v = REG[:, off // 4:(off + nb + 3) // 4]
            if dt != F32:
                v = v.bitcast(dt)[:, 0:n]
            v = v[0:shape[0], :]
            if len(shape) == 3:
                v = v.rearrange("p (a b) -> p a b", a=shape[1])
            elif len(shape) == 4:
                v = v.rearrange("p (a b c) -> p a b c", a=shape[1], b=shape[2])
            return v

        sems = {e: E(nc.semaphore("s_" + e)) for e in Sched.ENG}
        dsems = {q: [E(nc.semaphore("d_%s%d" % (q, i))) for i in range(8)] for q in ['sp', 'pool']}
        block = E(nc.Block())
        S = Sched(sems, dsems)

        IDB, O1024, OBLKB, O256, NEGMB, ONEB = [CB[:, i, :] for i in range(6)]

        def cf(i):
            return CF[:, i, :]

        def PK(i):
            return ('ps', i)

        def dbg(name, src_ap, shape, reads):
            if not DEBUG:
                return
            t_ = dout("dbg_" + name, shape)
            if len(shape) == 2:
                dst = t_[:, :]
            else:
                dst = t_[:, :, :]
            S.dma('sp', lambda e: e.dma_start(out=dst, in_=src_ap), reads=reads)

        def dbg_h(stage):
            dbg("hs_" + stage, H[:, :, NPR:NT], [128, KC, NS], [('H', 4)])
            dbg("hp_" + stage, H[:, :, 0:512], [128, KC, 512], [('H', 0)])

        def pe_group(mms, reads, writes):
            def fn(e, mms=mms):
                r = None
                for (o, l, rh, s0, s1) in mms:
                    r = e.matmul(o, l, rh, start=s0, stop=s1)
                return r
            S.op('pe', fn, reads=reads, writes=writes)

        def act(out, in_, func, reads, writes, bias=None, scale=None):
            kw = {}
            if bias is not None:
                kw['bias'] = bias
            if scale is not None:
                kw['scale'] = scale
            big = int(np.prod(out.shape[1:])) >= 256
            S.op('act', lambda e: e.activation(out=out, in_=in_, func=func, **kw), reads=reads, writes=writes, big=big)

        def dve(fn, reads, writes, big=False):
            S.op('dve', fn, reads=reads, writes=writes, big=big)

        wbi = [0]
        wb_lim = [len(WB)]

        def wload(parts):
            i = wbi[0] % wb_lim[0]
            wbi[0] += 1
            buf = WB[i]
            key = ('WB', i)
            for (dv, src) in parts:
                if NOW == 'noweights' and wbi[0] > 8:
                    continue
                S.dma('pool', lambda e, dv=dv, src=src, buf=buf: e.dma_start(out=dv(buf), in_=src), writes=[key])
            return buf, key

        def wslab(w2d, c0, ncols):
            src = w2d.rearrange("(k p) n -> p k n", p=128)[:, :, c0:c0 + ncols]
            vf = lambda buf, ncols=ncols: buf[:, 0:KC * ncols].rearrange("p (k n) -> p k n", k=KC)
            buf, key = wload([(vf, src)])
            return vf(buf), key

        def rstd_from_ps(ps_ap, n, out_ap, rkeys, wkeys):
            act(out_ap, ps_ap, AF.Ln, rkeys, wkeys, bias=EPSC[:ps_ap.shape[0], 0:1], scale=1.0)
            act(out_ap, out_ap, AF.Exp, wkeys, wkeys, scale=-0.5)

        PT_TILES = [(i * 512, 512, False, i) for i in range(4)]
        S_TILE = (NPR, NS, True, 4)

        def hk(t):
            return ('H', t[3])

        def modcol(l, sub, j, k):
            if l == 'kv':
                return MODKV[:, j * 8 + k, :]
            return MOD[:, l, (sub * 3 + j) * 8 + k, :]

        def norm_mod(t, l, sub, ucol, ukey):
            c0, n, samp, _ = t
            hv = H[:, :, c0:c0 + n]
            uv = U[:, :, ucol:ucol + n]
            act(uv, hv, AF.Square, [hk(t)], [ukey])
            pe_group([(PS[7][:, 0:n], O1024, U[:, k, ucol:ucol + n], k == 0, k == KC - 1) for k in range(KC)],
                     [ukey], [PK(7)])
            rstd_from_ps(PS[7][:, 0:n], n, RSTD[:, 0:n], [PK(7)], ['RSTD'])
            if not samp:
                for k in range(KC):
                    tk = ('TMP', k % 2)
                    dve(lambda e, k=k: e.scalar_tensor_tensor(out=TMP[:, k % 2, 0:n], in0=H[:, k, c0:c0 + n],
                                                              scalar=modcol(l, sub, 1, k)[:, 0:1], in1=RSTD[:, 0:n],
                                                              op0=ALU.mult, op1=ALU.mult),
                        [hk(t), 'RSTD', 'MOD'], [tk], big=True)
                    act(U[:, k, ucol:ucol + n], TMP[:, k % 2, 0:n], AF.Identity, [tk, 'MOD'], [ukey],
                        bias=modcol(l, sub, 0, k)[:, 0:1], scale=1.0)
            else:
                for k in range(KC):
                    tk = ('TMP', k % 2)
                    tv = TMP[:, k % 2, 0:n]
                    a_b = modcol(l, sub, 1, k)[:, 1:17].unsqueeze(2).to_broadcast([128, NSB, 4])
                    b_b = modcol(l, sub, 0, k)[:, 1:17].unsqueeze(2).to_broadcast([128, NSB, 4])
                    dve(lambda e, k=k, tv=tv: e.tensor_tensor(out=tv, in0=H[:, k, c0:c0 + n], in1=RSTD[:, 0:n], op=ALU.mult),
                        [hk(t), 'RSTD'], [tk])
                    dve(lambda e, tv=tv, a_b=a_b: e.tensor_tensor(out=tv.rearrange("p (b t) -> p b t", t=4),
                                                                  in0=tv.rearrange("p (b t) -> p b t", t=4), in1=a_b, op=ALU.mult),
                        [tk, 'MOD'], [tk])
                    dve(lambda e, k=k, tv=tv, b_b=b_b: e.tensor_tensor(
                        out=U[:, k, ucol:ucol + n].rearrange("p (b t) -> p b t", t=4),
                        in0=tv.rearrange("p (b t) -> p b t", t=4), in1=b_b, op=ALU.add),
                        [tk, 'MOD'], [ukey])

        def resid_add(t, l, sub, m, ps_ap, pskey):
            c0, n, samp, _ = t
            if not samp:
                dve(lambda e: e.scalar_tensor_tensor(out=H[:, m, c0:c0 + n], in0=ps_ap, scalar=modcol(l, sub, 2, m)[:, 0:1],
                                                     in1=H[:, m, c0:c0 + n], op0=ALU.mult, op1=ALU.add),
                    [pskey, hk(t), 'MOD'], [hk(t)], big=True)
            else:
                g_b = modcol(l, sub, 2, m)[:, 1:17].unsqueeze(2).to_broadcast([128, NSB, 4])
                tk = ('TMP', 0)
                dve(lambda e: e.tensor_tensor(out=TMP[:, 0, 0:n].rearrange("p (b t) -> p b t", t=4),
                                              in0=ps_ap.rearrange("p (b t) -> p b t", t=4), in1=g_b, op=ALU.mult),
                    [pskey, 'MOD'], [tk])
                dve(lambda e: e.tensor_tensor(out=H[:, m, c0:c0 + n], in0=H[:, m, c0:c0 + n], in1=TMP[:, 0, 0:n], op=ALU.add),
                    [tk, hk(t)], [hk(t)])

        def ld(dst, src, key, q='sp'):
            S.dma(q, lambda e: e.dma_start(out=dst, in_=src), writes=[key])

        ld(CF[:], cpack[:, :, :], 'CF')
        ld(BADA[:], b_adaT[:, :, :], 'small'); ld(GN[:], g_normT[:, :, :, :], 'small')
        ld(BAKV[:], b_akvT[:, :], 'small'); ld(GKV[:], g_kvT[:, :], 'small'); ld(GOUT[:], goutT[:, :], 'small')
        ld(BFB[:], bfb[:, :], 'small'); ld(GK[:], gk2[:, :], 'small'); ld(GQ[:], gq2[:, :], 'small')
        ld(FLG[:], flg[:, :], 'small'); ld(WG2[:], wg2b[:, :], 'small')
        ld(CTS[:], cT.rearrange("(k p) n -> p k n", p=128), 'CTS')
        dve(lambda e: e.memset(EPSC[:, 0:1], EPS), [], ['EPSC'])
        dve(lambda e: e.memset(EPSC[:, 1:2], 1.0), ['EPSC'], ['EPSC'])
        dve(lambda e: e.tensor_copy(out=IDB, in_=cf(C_IDENT)), ['CF'], ['CB'])
        dve(lambda e: e.tensor_scalar(out=O1024, in0=cf(C_ONES), scalar1=1.0 / 1024.0, scalar2=None, op0=ALU.mult), ['CF'], ['CB'])
        dve(lambda e: e.tensor_copy(out=OBLKB, in_=cf(C_OBLK)), ['CF'], ['CB'])
        dve(lambda e: e.tensor_scalar(out=O256, in0=cf(C_ONES), scalar1=1.0 / 256.0, scalar2=None, op0=ALU.mult), ['CF'], ['CB'])
        dve(lambda e: e.tensor_copy(out=NEGMB, in_=cf(C_NEGM)), ['CF'], ['CB'])
        dve(lambda e: e.tensor_copy(out=ONEB, in_=cf(C_ONES)), ['CF'], ['CB'])
        dve(lambda e: e.tensor_scalar(out=GQ[:], in0=GQ[:], scalar1=0.125, scalar2=None, op0=ALU.mult), ['small'], ['small'])
        S.barrier()

        act(SCT[:], CTS[:], AF.Silu, ['CTS'], ['SCT'])
        for l in range(2):
            for s6 in range(3):
                for sl in range(6):
                    slab = s6 * 6 + sl
                    wv, wk = wslab(w_ada[l], slab * 512, 512)
                    mms = []
                    for f4 in range(4):
                        fc = sl * 4 + f4
                        for k in range(KC):
                            mms.append((PS[s6][:, fc * 17:(fc + 1) * 17], wv[:, k, f4 * 128:(f4 + 1) * 128], SCT[:, k, :], k == 0, k == KC - 1))
                    pe_group(mms, [wk, 'SCT'], [PK(s6)])
                fc0 = s6 * 24
                dve(lambda e, l=l, s6=s6, fc0=fc0: e.tensor_tensor(
                    out=MOD[:, l, fc0:fc0 + 24, :], in0=PS[s6][:, 0:24 * 17].rearrange("p (f s) -> p f s", s=17),
                    in1=BADA[:, l, fc0:fc0 + 24].unsqueeze(2).to_broadcast([128, 24, 17]), op=ALU.add),
                    [PK(s6), 'small'], ['MOD'])
        for sl in range(4):
            wv, wk = wslab(w_akv, sl * 512, 512)
            mms = []
            for f4 in range(4):
                fc = sl * 4 + f4
                for k in range(KC):
                    mms.append((PS[3][:, fc * 17:(fc + 1) * 17], wv[:, k, f4 * 128:(f4 + 1) * 128], SCT[:, k, :], k == 0, k == KC - 1))
            pe_group(mms, [wk, 'SCT'], [PK(3)])
        dve(lambda e: e.tensor_tensor(out=MODKV[:], in0=PS[3][:, 0:16 * 17].rearrange("p (f s) -> p f s", s=17),
                                      in1=BAKV[:].unsqueeze(2).to_broadcast([128, 16, 17]), op=ALU.add), [PK(3), 'small'], ['MOD'])
        for l in range(2):
            for sub in range(3):
                f0 = (sub * 3 + 1) * 8
                dve(lambda e, l=l, sub=sub, f0=f0: e.scalar_tensor_tensor(
                    out=MOD[:, l, f0:f0 + 8, :], in0=MOD[:, l, f0:f0 + 8, :], scalar=1.0,
                    in1=GN[:, l, sub, :].unsqueeze(2).to_broadcast([128, 8, 17]), op0=ALU.add, op1=ALU.mult), ['MOD', 'small'], ['MOD'])
                if sub != 1:
                    g0 = (sub * 3 + 2) * 8
                    dve(lambda e, l=l, g0=g0: e.tensor_scalar(out=MOD[:, l, g0:g0 + 8, :], in0=MOD[:, l, g0:g0 + 8, :],
                                                              scalar1=0.5, scalar2=None, op0=ALU.mult), ['MOD'], ['MOD'])
        dve(lambda e: e.scalar_tensor_tensor(out=MODKV[:, 8:16, :], in0=MODKV[:, 8:16, :], scalar=1.0,
                                             in1=GKV[:].unsqueeze(2).to_broadcast([128, 8, 17]), op0=ALU.add, op1=ALU.mult),
            ['MOD', 'small'], ['MOD'])
        S.barrier()

        def ffn(l, which, sub, tiles):
            if True:
                HID = carve(0, [128, JC, 1088], BF16)
                SA = carve(JC * 1088 * 2, [128, 2, 512], F32)
                ucols = []
                uc = 0
                for t in tiles:
                    ucols.append(uc)
                    norm_mod(t, l, sub, uc, ('U', t[3]))
                    uc += t[1]
                wu = w_up[l, which]
                wd = w_dn[l, which]
                pi = [0]
                for j2 in range(JC // 2):
                    srca = wu.rearrange("(k p) n -> p k n", p=128)[:, :, j2 * 256:(j2 + 1) * 256]
                    srcb = wu.rearrange("(k p) n -> p k n", p=128)[:, :, DFF + j2 * 256:DFF + (j2 + 1) * 256]
                    va = lambda buf: buf[:, 0:2048].rearrange("p (k n) -> p k n", k=KC)
                    vb = lambda buf: buf[:, 2048:4096].rearrange("p (k n) -> p k n", k=KC)
                    buf, wk = wload([(va, srca), (vb, srcb)])
                    wa, wbb = va(buf), vb(buf)
                    for jj in range(2):
                        j = j2 * 2 + jj
                        for ti, t in enumerate(tiles):
                            n = t[1]
                            u0 = ucols[ti]
                            pa = pi[0] % 3
                            pb = 3 + pi[0] % 3
                            pi[0] += 1
                            pe_group([(PS[pa][:, 0:n], wa[:, k, jj * 128:(jj + 1) * 128], U[:, k, u0:u0 + n], k == 0, k == KC - 1) for k in range(KC)],
                                     [wk, ('U', t[3])], [PK(pa)])
                            pe_group([(PS[pb][:, 0:n], wbb[:, k, jj * 128:(jj + 1) * 128], U[:, k, u0:u0 + n], k == 0, k == KC - 1) for k in range(KC)],
                                     [wk, ('U', t[3])], [PK(pb)])
                            sk = ('SA', pa % 2)
                            act(SA[:, pa % 2, 0:n], PS[pa][:, 0:n], AF.Silu, [PK(pa)], [sk])
                            dve(lambda e, pa=pa, pb=pb, j=j, u0=u0, n=n: e.tensor_tensor(
                                out=HID[:, j, u0:u0 + n], in0=SA[:, pa % 2, 0:n], in1=PS[pb][:, 0:n], op=ALU.mult),
                                [sk, PK(pb)], [('HID', t[3])], big=(n >= 256))
                for m in range(KC):
                    src = wd.rearrange("(j p) n -> p j n", p=128)[:, :, m * 128:(m + 1) * 128]
                    vd = lambda buf: buf[:, 0:JC * 128].rearrange("p (j n) -> p j n", j=JC)
                    buf, wk = wload([(vd, src)])
                    wdv = vd(buf)
                    for ti, t in enumerate(tiles):
                        n = t[1]
                        u0 = ucols[ti]
                        pd = 6 + pi[0] % 2
                        pi[0] += 1
                        pe_group([(PS[pd][:, 0:n], wdv[:, j, :], HID[:, j, u0:u0 + n], j == 0, j == JC - 1) for j in range(JC)],
                                 [wk, ('HID', t[3])], [PK(pd)])
                        resid_add(t, l, sub, m, PS[pd][:, 0:n], PK(pd))
                S.barrier()

        class RA:
            def __init__(self):
                self.off = 0

            def get(self, shape, dt):
                n = int(np.prod(shape[1:])) * (4 if dt == F32 else 2)
                n = (n + 3) // 4 * 4
                v = carve(self.off, shape, dt)
                self.off += n
                return v

        def gla_tile(t, bar=True):
            c0, n, samp, idx = t
            C = 64 if samp else 128
            nblk = n // C
            ra = RA()
            QT = ra.get([128, 4, n], BF16); KT = ra.get([128, 4, n], BF16); SR = ra.get([128, 8, n], BF16)
            KTMA = ra.get([C, nblk, 512], BF16); VTMA = ra.get([C, nblk, 1024], BF16)
            GP = ra.get([C, 512], F32); EQ = ra.get([128, 4, C], F32); EK = ra.get([128, 4, C], F32)
            QS = ra.get([128, 4, C], BF16); KS = ra.get([128, 4, C], BF16); KH = ra.get([C, 512], BF16)
            ATM = ra.get([C, 4, C], BF16); RSO = ra.get([128, 4, C], F32); TO = ra.get([128, 8, C], F32)
            EB = ra.get([C, 512], F32); SQO = ra.get([128, 8, C], BF16)
            GLRT = ra.get([17, 512], F32)
            if samp:
                VBD = ra.get([64, 16, 256], BF16)
                S0B = [ra.get([128, 2, 4, 256], F32) for _ in range(2)]
                S0BF = [ra.get([128, 2, 4, 256], BF16) for _ in range(2)]
            ukey = ('U', idx)
            norm_mod(t, 0, 1, 0, ukey)
            pr = [0]

            def nextp():
                p = pr[0] % 4
                pr[0] += 1
                return p

            def fm_proj(wv, wk, cc, dst, func=AF.Copy, dkey=None):
                p = nextp()
                pe_group([(PS[p][:, 0:n], wv[:, k, cc * 128:(cc + 1) * 128], U[:, k, 0:n], k == 0, k == KC - 1) for k in range(KC)],
                         [wk, ukey], [PK(p)])
                act(dst, PS[p][:, 0:n], func, [PK(p)], [dkey])

            def tm_proj(wv, wk, b, dst, dkey):
                p = nextp()
                pe_group([(PS[p][0:C, 0:512], U[:, k, b * C:(b + 1) * C], wv[:, k, 0:512], k == 0, k == KC - 1) for k in range(KC)],
                         [wk, ukey], [PK(p)])
                dve(lambda e, p=p, dst=dst: e.tensor_copy(out=dst, in_=PS[p][0:C, 0:512]), [PK(p)], [dkey], big=True)

            wv, wk = wslab(w_in, 0, 512)
            for h in range(4):
                fm_proj(wv, wk, h, QT[:, h, :], dkey='QT')
            wv, wk = wslab(w_in, 512, 512)
            for h in range(4):
                fm_proj(wv, wk, h, KT[:, h, :], dkey='KT')
            for b in range(nblk):
                tm_proj(wv, wk, b, KTMA[:, b, :], 'KTMA')
            for half in range(2):
                wv, wk = wslab(w_in, 1024 + half * 512, 512)
                for b in range(nblk):
                    tm_proj(wv, wk, b, VTMA[:, b, half * 512:(half + 1) * 512], 'VTMA')
            for half in range(2):
                wv, wk = wslab(w_in, 2048 + half * 512, 512)
                for cc in range(4):
                    c8 = half * 4 + cc
                    fm_proj(wv, wk, cc, SR[:, c8, :], func=AF.Silu, dkey='SR')
            for ec in range(2):
                dve(lambda e, ec=ec: e.tensor_scalar(out=SR[:, ec::2, :], in0=SR[:, ec::2, :], scalar1=GOUT[:, ec:ec + 1], scalar2=None, op0=ALU.mult),
                    ['SR', 'small'], ['SR'])
            wv, wk = wslab(w_in, 3072, 16)
            p = nextp()
            dve(lambda e: e.memset(GLRT[:, 0:n], 1.0), [], ['GLRT'])
            pe_group([(PS[p][0:16, 0:n], wv[:, k, 0:16], U[:, k, 0:n], k == 0, k == KC - 1) for k in range(KC)], [wk, ukey], [PK(p)])
            act(GLRT[0:16, 0:n], PS[p][0:16, 0:n], AF.Copy, [PK(p)], ['GLRT'])

            TRIN = cf(C_TRINS if samp else C_TRIN)[0:C, 0:C]
            TRUN = cf(C_TRUNS if samp else C_TRUN)[0:C, 0:C]
            M01 = cf(C_M01S if samp else C_M01)[0:C, 0:C]
            for b in range(nblk):
                bs = slice(b * C, (b + 1) * C)
                pe_group([(PS[4][0:C, 0:512], GLRT[0:17, bs], WG2[0:17, 0:512], True, True)], ['GLRT', 'small'], [PK(4)])
                act(GP[:, :], PS[4][0:C, 0:512], AF.Exp, [PK(4)], ['GP'], scale=-1.0)
                act(GP[:, :], GP[:, :], AF.Ln, ['GP'], ['GP'], bias=EPSC[0:C, 1:2], scale=1.0)
                pe_group([(PS[5][:, h * C:(h + 1) * C], GP[:, h * 128:(h + 1) * 128], TRIN, True, True) for h in range(4)], ['GP', 'CF'], [PK(5)])
                pe_group([(PS[6][0:C, 0:512], TRUN, GP[:, :], True, True)], ['GP', 'CF'], [PK(6)])
                psb = PS[5][:, 0:4 * C].rearrange("p (h c) -> p h c", h=4)
                act(EQ[:], psb, AF.Exp, [PK(5)], ['EQ'])
                act(EK[:], psb, AF.Exp, [PK(5)], ['EK'], scale=-1.0)
                act(EB[:, :], PS[6][0:C, 0:512], AF.Exp, [PK(6)], ['EB'])
                dve(lambda e, bs=bs: e.scalar_tensor_tensor(out=QS[:], in0=QT[:, :, bs], scalar=128.0 ** -0.5, in1=EQ[:], op0=ALU.mult, op1=ALU.mult),
                    ['QT', 'EQ'], ['QS'])
                dve(lambda e, bs=bs: e.tensor_tensor(out=KS[:], in0=KT[:, :, bs], in1=EK[:], op=ALU.mult), ['KT', 'EK'], ['KS'])
                dve(lambda e, b=b: e.tensor_tensor(out=KH[:, :], in0=KTMA[:, b, :], in1=EB[:, :], op=ALU.mult), ['KTMA', 'EB'], ['KH'])
                pe_group([(PS[4][0:C, h * C:(h + 1) * C], KS[:, h, :], QS[:, h, :], True, True) for h in range(4)], ['KS', 'QS'], [PK(4)])
                dve(lambda e: e.tensor_tensor(out=ATM[:], in0=PS[4][0:C, 0:4 * C].rearrange("p (h c) -> p h c", h=4),
                                              in1=M01.unsqueeze(1).to_broadcast([C, 4, C]), op=ALU.mult), [PK(4), 'CF'], ['ATM'])

                def obank(hec):
                    if C == 128:
                        return PS[hec // 4][:, (hec % 4) * C:(hec % 4 + 1) * C]
                    return PS[0][:, hec * C:(hec + 1) * C]
                obanks = [PK(0), PK(1)] if C == 128 else [PK(0)]
                mms = []
                for h in range(4):
                    for ec in range(2):
                        hec = h * 2 + ec
                        mms.append((obank(hec), VTMA[:, b, h * 256 + ec * 128:h * 256 + (ec + 1) * 128], ATM[:, h, :], (not samp) or hec == 0, False))
                        if not samp:
                            mms.append((obank(hec), SBF[:, h, ec * 128:(ec + 1) * 128], QS[:, h, :], False, True))
                pe_group(mms, ['VTMA', 'ATM', 'SBF', 'QS'], obanks)
                if samp and DEBUG:
                    DB1 = TO[:].rearrange("p a c -> p (a c)"); DB2 = DB1; DB3 = RSO[:].rearrange("p a c -> p (a c)")[0:64, :]
                    dve(lambda e: e.tensor_copy(out=DB1[:, :], in_=PS[0][:, 0:512]), [PK(0)], ['TO'])
                    dbg("ot_intra", DB1[:, :], [128, 512], ['TO'])
                    dve(lambda e: e.tensor_copy(out=DB3[:, :], in_=ATM[:].rearrange("p h c -> p (h c)")), ['ATM'], ['RSO'])
                    dbg("atm", DB3[:, :], [64, 256], ['RSO'])
                if not samp:
                    pe_group([(PS[2 + h // 2][:, (h % 2) * 256:(h % 2 + 1) * 256], KH[:, h * 128:(h + 1) * 128], VTMA[:, b, h * 256:(h + 1) * 256], True, True)
                              for h in range(4)], ['KH', 'VTMA'], [PK(2), PK(3)])
                    for h in range(4):
                        dve(lambda e, h=h: e.scalar_tensor_tensor(out=SST[:, h, :], in0=SST[:, h, :], scalar=EQ[:, h, C - 1:C],
                                                                  in1=PS[2 + h // 2][:, (h % 2) * 256:(h % 2 + 1) * 256], op0=ALU.mult, op1=ALU.add),
                            ['SST', 'EQ', PK(2 + h // 2)], ['SST'])
                    act(SBF[:], SST[:], AF.Copy, ['SST'], ['SBF'])
                else:
                    for bp in range(NSB // 2):
                        sbuf = S0B[bp % 2]
                        skey = ('S0B', bp % 2)
                        S.dma('sp', lambda e, bp=bp, sbuf=sbuf: e.dma_start(out=sbuf[:, 0, :, :], in_=st0[2 * bp].rearrange("h d e -> d h e")), writes=[skey])
                        S.dma('sp', lambda e, bp=bp, sbuf=sbuf: e.dma_start(out=sbuf[:, 1, :, :], in_=st0[2 * bp + 1].rearrange("h d e -> d h e")), writes=[skey])
                        sbf = S0BF[bp % 2]
                        sbk = ('S0BF', bp % 2)
                        act(sbf[:].rearrange("p a h e -> p (a h e)"), sbuf[:].rearrange("p a h e -> p (a h e)"), AF.Copy, [skey], [sbk])
                        mms = []
                        for bb in range(2):
                            sq = 2 * bp + bb
                            for h in range(4):
                                for ec in range(2):
                                    hec = h * 2 + ec
                                    last = (bp == NSB // 2 - 1 and bb == 1)
                                    mms.append((PS[0][:, hec * C + 4 * sq:hec * C + 4 * sq + 4], sbf[:, bb, h, ec * 128:(ec + 1) * 128],
                                                QS[:, h, 4 * sq:4 * sq + 4], False, last))
                        pe_group(mms, [sbk, 'QS'], [PK(0)])
                        for h in range(4):
                            vk = 'VBD'
                            if bp == 0 or True:
                                pass
                            dve(lambda e, h=h, bp=bp: e.tensor_tensor(
                                out=VBD[:, 0:2, :], in0=VTMA[:, 0, h * 256:(h + 1) * 256].unsqueeze(1).to_broadcast([64, 2, 256]),
                                in1=cf(C_MB)[0:64, 2 * bp:2 * bp + 2].unsqueeze(2).to_broadcast([64, 2, 256]), op=ALU.mult),
                                ['VTMA', 'CF'], [vk])
                            pq = 2 + (bp * 4 + h) % 2
                            pe_group([(PS[pq][:, 0:512], KH[:, h * 128:(h + 1) * 128], VBD[:, 0:2, :].rearrange("p a b -> p (a b)"), True, True)],
                                     ['KH', vk], [PK(pq)])
                            for bb in range(2):
                                sq = 2 * bp + bb
                                dve(lambda e, h=h, bb=bb, sq=sq, pq=pq, sbuf=sbuf: e.scalar_tensor_tensor(
                                    out=sbuf[:, bb, h, :], in0=sbuf[:, bb, h, :], scalar=EQ[:, h, 4 * sq + 3:4 * sq + 4],
                                    in1=PS[pq][:, bb * 256:(bb + 1) * 256], op0=ALU.mult, op1=ALU.add), [skey, 'EQ', PK(pq)], [skey])
                        for bb in range(2):
                            S.dma('sp', lambda e, bp=bp, bb=bb, sbuf=sbuf: e.dma_start(out=sgs[2 * bp + bb].rearrange("h d e -> d h e"), in_=sbuf[:, bb, :, :]),
                                  reads=[skey])
                if samp and DEBUG:
                    dve(lambda e: e.tensor_copy(out=DB2[:, :], in_=PS[0][:, 0:512]), [PK(0)], ['TO'])
                    dbg("ot_full", DB2[:, :], [128, 512], ['TO'])
                for bk in range(len(obanks)):
                    w = 4 * C if C == 128 else 8 * C
                    nq = 4 if C == 128 else 8
                    act(SQO[:, bk * 4:bk * 4 + nq, :], PS[bk][:, 0:w].rearrange("p (a c) -> p a c", a=nq), AF.Square, [obanks[bk]], ['SQO'])
                mms = []
                for h in range(4):
                    for ec in range(2):
                        mms.append((PS[6][:, h * C:(h + 1) * C], O256, SQO[:, h * 2 + ec, :], ec == 0, ec == 1))
                pe_group(mms, ['SQO'], [PK(6)])
                rstd_from_ps(PS[6][:, 0:4 * C], 4 * C, RSO[:].rearrange("p h c -> p (h c)"), [PK(6)], ['RSO'])
                for bk in range(len(obanks)):
                    nh = 2 if C == 128 else 4
                    w = 4 * C if C == 128 else 8 * C
                    dve(lambda e, bk=bk, nh=nh, w=w: e.tensor_tensor(
                        out=TO[:, bk * 4:bk * 4 + 2 * nh, :].rearrange("p (h x) c -> p h x c", x=2),
                        in0=PS[bk][:, 0:w].rearrange("p (h x c) -> p h x c", h=nh, x=2),
                        in1=RSO[:, bk * 2:bk * 2 + nh, :].unsqueeze(2).to_broadcast([128, nh, 2, C]), op=ALU.mult),
                        [obanks[bk], 'RSO'], ['TO'])
                if samp:
                    dbg("gla_to", TO[:, :, :], [128, 8, 64], ['TO'])
                    dbg("gla_sr0", SR[:, :, :], [128, 8, 64], ['SR']) if False else None
                dve(lambda e, bs=bs: e.tensor_tensor(out=SR[:, :, bs], in0=TO[:], in1=SR[:, :, bs], op=ALU.mult), ['TO', 'SR'], ['SR'])
            for half in range(2):
                wv, wk = wslab(w_out, half * 512, 512)
                for mm_ in range(4):
                    m = half * 4 + mm_
                    p = 4 + m % 2
                    pe_group([(PS[p][:, 0:n], wv[:, c8, mm_ * 128:(mm_ + 1) * 128], SR[:, c8, :], c8 == 0, c8 == KC - 1) for c8 in range(KC)],
                             [wk, 'SR'], [PK(p)])
                    resid_add(t, 0, 1, m, PS[p][:, 0:n], PK(p))
            if bar:
                S.barrier()

        KNS = sb("KNS", [128, KC, NS], BF16)
        VNS = sb("VNS", [NS, D], BF16)

        def kv_tile(t, koff, emit_out, bar=True):
            c0, n, samp, idx = t
            C = 64 if samp else 128
            nblk = n // C
            ra = RA()
            KF = [ra.get([128, n], F32) for _ in range(2)]
            SQK = ra.get([128, n], BF16); RSK = ra.get([128, n], F32)
            KN = [ra.get([128, n], F32) for _ in range(2)]
            KNB = [ra.get([128, n], BF16) for _ in range(2)]
            VF = ra.get([C, nblk, 1024], F32)
            VA = [ra.get([C, 16, 128], BF16) for _ in range(2)]
            LX = ra.get([C, 16], F32)
            ukey = ('U', idx)
            norm_mod(t, 'kv', None, 0, ukey)
            for i in range(2):
                dve(lambda e, i=i: e.memset(VA[i][:, :, 64:128], 1.0), [], [('VA', i)])
            for half in range(2):
                wv, wk = wslab(w_kvf, half * 512, 512)
                for cc in range(4):
                    c8 = half * 4 + cc
                    p = c8 % 2
                    r = c8 % 2
                    pe_group([(PS[p][:, 0:n], wv[:, k, cc * 128:(cc + 1) * 128], U[:, k, 0:n], k == 0, k == KC - 1) for k in range(KC)],
                             [wk, ukey], [PK(p)])
                    act(KF[r][:, :], PS[p][:, 0:n], AF.Copy, [PK(p)], [('KF', r)])
                    act(SQK[:, :], KF[r][:, :], AF.Square, [('KF', r)], ['SQK'])
                    pe_group([(PS[2][:, 0:n], OBLKB, SQK[:, :], True, True)], ['SQK'], [PK(2)])
                    rstd_from_ps(PS[2][:, 0:n], n, RSK[:, :], [PK(2)], ['RSK'])
                    dve(lambda e, r=r: e.scalar_tensor_tensor(out=KN[r][:, :], in0=KF[r][:, :], scalar=GK[:, 0:1], in1=RSK[:, :], op0=ALU.mult, op1=ALU.mult),
                        [('KF', r), 'RSK', 'small'], [('KN', r)])
                    if samp:
                        S.dma('sp', lambda e, r=r, c8=c8: e.dma_start(out=kso[c8 * 128:(c8 + 1) * 128, :], in_=KN[r][:, :]), reads=[('KN', r)])
                        dve(lambda e, r=r, c8=c8: e.tensor_copy(out=KNS[:, c8, :], in_=KN[r][:, :]), [('KN', r)], ['KNS'])
                    else:
                        if emit_out:
                            S.dma('sp', lambda e, r=r, c8=c8: e.dma_start(out=ko[c8 * 128:(c8 + 1) * 128, c0:c0 + n], in_=KN[r][:, :]), reads=[('KN', r)])
                        act(KNB[r][:, :], KN[r][:, :], AF.Copy, [('KN', r)], [('KNB', r)])
                        for hh in range(2):
                            S.dma('sp', lambda e, r=r, c8=c8, hh=hh: e.dma_start(out=KTD[2 * c8 + hh, 0:64, koff + c0:koff + c0 + n],
                                                                                 in_=KNB[r][hh * 64:(hh + 1) * 64, :]),
                                  reads=[('KNB', r)], writes=['KTD'])
            for half in range(2):
                wv, wk = wslab(w_kvf, 1024 + half * 512, 512)
                for b in range(nblk):
                    p = 3 + b % 2
                    pe_group([(PS[p][0:C, 0:512], U[:, k, b * C:(b + 1) * C], wv[:, k, 0:512], k == 0, k == KC - 1) for k in range(KC)],
                             [wk, ukey], [PK(p)])
                    act(VF[:, b, half * 512:(half + 1) * 512], PS[p][0:C, 0:512], AF.Copy, [PK(p)], [('VF', b)])
            wv, wk = wslab(w_kvf, 2048, 16)
            for b in range(nblk):
                blk = 32 if samp else (koff + c0) // 128 + b
                r = b % 2
                if samp:
                    S.dma('sp', lambda e: e.dma_start(out=vso[:, :], in_=VF[:, 0, :]), reads=[('VF', 0)])
                    dve(lambda e: e.tensor_copy(out=VNS[:, :], in_=VF[:, 0, :]), [('VF', 0)], ['VNS'])
                else:
                    if emit_out:
                        S.dma('sp', lambda e, b=b: e.dma_start(out=vo[c0 + b * 128:c0 + (b + 1) * 128, :], in_=VF[:, b, :]), reads=[('VF', b)])
                    dve(lambda e, b=b, r=r: e.tensor_copy(out=VA[r][:, :, 0:64], in_=VF[:, b, :].rearrange("p (h x) -> p h x", h=16)),
                        [('VF', b)], [('VA', r)])
                    S.dma('sp', lambda e, b=b, r=r: e.dma_start(out=VD[koff + c0 + b * 128:koff + c0 + (b + 1) * 128, :, :], in_=VA[r][:, :, :]),
                          reads=[('VA', r)], writes=['VD'])
                pe_group([(PS[5][0:C, 0:16], U[:, k, b * C:(b + 1) * C], wv[:, k, 0:16], k == 0, k == KC - 1) for k in range(KC)], [wk, ukey], [PK(5)])
                dve(lambda e: e.tensor_tensor(out=LX[:, :], in0=PS[5][0:C, 0:16], in1=BFB[0:C, :], op=ALU.add), [PK(5), 'small'], ['LX'])
                act(LX[:, :], LX[:, :], AF.Exp, ['LX'], ['LX'], scale=-1.0)
                act(LX[:, :], LX[:, :], AF.Ln, ['LX'], ['LX'], bias=EPSC[0:C, 1:2], scale=1.0)
                dve(lambda e, blk=blk: e.tensor_scalar(out=LFT[0:C, blk, :], in0=LX[:, :], scalar1=-1.0, scalar2=None, op0=ALU.mult), ['LX'], ['LFT'])
                if samp:
                    S.dma('sp', lambda e, blk=blk: e.dma_start(out=lfso[:, :], in_=LFT[0:C, blk, :]), reads=['LFT'])
                elif emit_out:
                    S.dma('sp', lambda e, b=b, blk=blk: e.dma_start(out=lfo[c0 + b * 128:c0 + (b + 1) * 128, :], in_=LFT[:, blk, :]), reads=['LFT'])
            if bar:
                S.barrier()

        def f_pipeline():
            ra = RA()
            LFE = ra.get([128, 32, 16], F32); TOTS = ra.get([128, 32, 16], F32); CAR = ra.get([128, 32, 16], F32)
            FTM = ra.get([128, 32, 16], F32); NFK = ra.get([128, 32, 16], F32)
            FTX = ra.get([16, 2048], F32); R1 = ra.get([16, 2048], F32)
            SPL = ra.get([16, 3, 2048], BF16); ONE3 = ra.get([16, 3, 2048], BF16)
            dve(lambda e: e.tensor_scalar(out=LFE[:, 0:16, :], in0=LFT[:, 0:16, :], scalar1=FLG[:, 0:1], scalar2=None, op0=ALU.mult), ['LFT', 'small'], ['LFE'])
            dve(lambda e: e.tensor_copy(out=LFE[:, 16:32, :], in_=LFT[:, 16:32, :]), ['LFT'], ['LFE'])
            dve(lambda e: e.memset(ONE3[:], 1.0), [], ['ONE3'])
            lfe = LFE[:].rearrange("p b h -> p (b h)")
            pe_group([(PS[0][:, 0:512], cf(C_M01), lfe, True, True)], ['LFE', 'CF'], [PK(0)])
            pe_group([(PS[1][:, 0:512], cf(C_ONES), lfe, True, True)], ['LFE', 'CF'], [PK(1)])
            act(TOTS[:].rearrange("p b h -> p (b h)"), PS[1][:, 0:512], AF.Copy, [PK(1)], ['TOTS'])
            dve(lambda e: e.memset(CAR[:, 0, :], 0.0), [], ['CAR'])
            for b in range(1, 32):
                dve(lambda e, b=b: e.tensor_tensor(out=CAR[:, b, :], in0=CAR[:, b - 1, :], in1=TOTS[:, b - 1, :], op=ALU.add), ['CAR', 'TOTS'], ['CAR'])
            dve(lambda e: e.tensor_tensor(out=FTM[:].rearrange("p b h -> p (b h)"), in0=PS[0][:, 0:512], in1=CAR[:].rearrange("p b h -> p (b h)"), op=ALU.add),
                [PK(0), 'CAR'], ['FTM'])
            dve(lambda e: e.tensor_scalar(out=NFK[:, 0:16, :], in0=FTM[:, 0:16, :], scalar1=FLG[:, 1:2], scalar2=-1.0, op0=ALU.add, op1=ALU.mult), ['FTM', 'small'], ['NFK'])
            dve(lambda e: e.tensor_scalar(out=NFK[:, 16:32, :], in0=FTM[:, 16:32, :], scalar1=-1.0, scalar2=None, op0=ALU.mult), ['FTM'], ['NFK'])

            def split_and_store(src, blk0, dsts):
                for g in range(4):
                    pe_group([(PS[2 + g % 2][0:16, j * 128:(j + 1) * 128], src[:, blk0 + g * 4 + j, :], cf(C_IDENT), True, True) for j in range(4)],
                             ['FTM', 'NFK', 'CF'], [PK(2 + g % 2)])
                    act(FTX[:, g * 512:(g + 1) * 512], PS[2 + g % 2][0:16, 0:512], AF.Copy, [PK(2 + g % 2)], ['FTX'])
                act(SPL[:, 0, :], FTX[:, :], AF.Copy, ['FTX'], ['SPL'])
                dve(lambda e: e.tensor_tensor(out=R1[:, :], in0=FTX[:, :], in1=SPL[:, 0, :], op=ALU.subtract), ['FTX', 'SPL'], ['R1'])
                act(SPL[:, 1, :], R1[:, :], AF.Copy, ['R1'], ['SPL'])
                dve(lambda e: e.tensor_tensor(out=R1[:, :], in0=R1[:, :], in1=SPL[:, 1, :], op=ALU.subtract), ['R1', 'SPL'], ['R1'])
                act(SPL[:, 2, :], R1[:, :], AF.Copy, ['R1'], ['SPL'])
                (d_spl, d_one, dkey) = dsts
                S.dma('sp', lambda e: e.dma_start(out=d_spl, in_=SPL[:, :, :]), reads=['SPL'], writes=[dkey])
                S.dma('sp', lambda e: e.dma_start(out=d_one, in_=ONE3[:, :, :]), reads=['ONE3'], writes=[dkey])

            split_and_store(NFK, 0, (KTD[:, 67:70, 0:2048], KTD[:, 64:67, 0:2048], 'KTD'))
            split_and_store(NFK, 16, (KTD[:, 67:70, 2048:4096], KTD[:, 64:67, 2048:4096], 'KTD'))
            split_and_store(FTM, 16, (QD[:, 64:67, 0:2048], QD[:, 67:70, 0:2048], 'QD'))
            dbg("ftm", FTM[:, :, :], [128, 32, 16], ['FTM'])
            dbg("nfk", NFK[:, :, :], [128, 32, 16], ['NFK'])
            dbg("ftx", FTX[:, :], [16, 2048], ['FTX'])
            dbg("r1", R1[:, :], [16, 2048], ['R1'])
            SPF = CAR[:].rearrange("p b h -> p (b h)")[0:16, :]
            for i3 in range(3):
                dve(lambda e, i3=i3: e.tensor_copy(out=SPF, in_=SPL[:, i3, 0:512]), ['SPL', 'SPF'], ['SPF'])
                dbg("spf%d" % i3, SPF, [16, 512], ['SPF'])
            S.barrier()

        RALL = [WB[2][:, :].bitcast(F32), WB[3][:, :].bitcast(F32)]
        IDXG = MODKV[:].rearrange("p a b -> p (a b)")[:, 0:NSB * NPG].bitcast(I32)
        IOTG = BFB[:, 0:1].bitcast(I32)

        def rb_view(b):
            return RALL[b // 8][:, (b % 8) * 256:(b % 8 + 1) * 256].rearrange("p (a h) -> p a h", a=NPG)

        def rb_key(b):
            return ('WB', 2 + b // 8)

        def emit_R_gather(b, pgs, lp):
            lk = 'LP1'
            for pg in pgs:
                i = b * NPG + pg
                S.dma('pool', lambda e, i=i, pg=pg: e.indirect_dma_start(out=lp[:, pg, :], out_offset=None, in_=clf[:, :],
                                                                         in_offset=bass.IndirectOffsetOnAxis(ap=IDXG[:, i:i + 1], axis=0)),
                      reads=['IDX'], writes=[lk])

        def emit_R_compute(b, lp):
            lk = 'LP1'
            RB = rb_view(b)
            rk = rb_key(b)
            lpf = lp[:].rearrange("p a h -> p (a h)")
            pe_group([(PS[7][:, 0:256], cf(C_USTR), lpf, True, True), (PS[7][:, 256:512], cf(C_ONES), lpf, False, True)], [lk, 'CF'], [PK(7)])
            dve(lambda e: e.tensor_copy(out=RB.rearrange("p a h -> p (a h)"), in_=PS[7][:, 0:256]), [PK(7)], [rk], big=True)
            for p2 in range(1, NPG):
                dve(lambda e, p2=p2: e.tensor_tensor(out=RB[:, 0:p2, :], in0=RB[:, 0:p2, :],
                                                     in1=PS[7][:, 256 + p2 * 16:256 + (p2 + 1) * 16].unsqueeze(1).to_broadcast([128, p2, 16]), op=ALU.add),
                    [rk, PK(7)], [rk])

        def qg_proj(t, ra, qsink, SG):
            c0, n, samp, idx = t
            QF = [ra.get([128, n], F32) for _ in range(2)]
            SQ = ra.get([128, n], BF16); RSQ = ra.get([128, n], F32)
            QNB = [ra.get([128, n], BF16) for _ in range(2)]
            ukey = ('U', idx)
            norm_mod(t, 1, 1, 0, ukey)
            for half in range(2):
                wv, wk = wslab(w_qg, half * 512, 512)
                for cc in range(4):
                    c8 = half * 4 + cc
                    p = c8 % 2
                    r = c8 % 2
                    pe_group([(PS[p][:, 0:n], wv[:, k, cc * 128:(cc + 1) * 128], U[:, k, 0:n], k == 0, k == KC - 1) for k in range(KC)],
                             [wk, ukey], [PK(p)])
                    act(QF[r][:, :], PS[p][:, 0:n], AF.Copy, [PK(p)], [('QF', r)])
                    act(SQ[:, :], QF[r][:, :], AF.Square, [('QF', r)], ['SQ'])
                    pe_group([(PS[2][:, 0:n], OBLKB, SQ[:, :], True, True)], ['SQ'], [PK(2)])
                    rstd_from_ps(PS[2][:, 0:n], n, RSQ[:, :], [PK(2)], ['RSQ'])
                    dve(lambda e, r=r: e.scalar_tensor_tensor(out=QNB[r][:, :], in0=QF[r][:, :], scalar=GQ[:, 0:1], in1=RSQ[:, :], op0=ALU.mult, op1=ALU.mult),
                        [('QF', r), 'RSQ', 'small'], [('QNB', r)])
                    qsink(c8, QNB[r], ('QNB', r))
            for half in range(2):
                wv, wk = wslab(w_qg, 1024 + half * 512, 512)
                for cc in range(4):
                    c8 = half * 4 + cc
                    p = 3 + c8 % 2
                    pe_group([(PS[p][:, 0:n], wv[:, k, cc * 128:(cc + 1) * 128], U[:, k, 0:n], k == 0, k == KC - 1) for k in range(KC)],
                             [wk, ukey], [PK(p)])
                    act(SG[:, c8, :], PS[p][:, 0:n], AF.Sigmoid, [PK(p)], ['SG'])

        def wo_proj(t, OG):
            c0, n, samp, idx = t
            for half in range(2):
                wv, wk = wslab(w_o, half * 512, 512)
                for mm_ in range(4):
                    m = half * 4 + mm_
                    p = 6 + m % 2
                    pe_group([(PS[p][:, 0:n], wv[:, c8, mm_ * 128:(mm_ + 1) * 128], OG[:, c8, :], c8 == 0, c8 == KC - 1) for c8 in range(KC)],
                             [wk, 'OG'], [PK(p)])
                    resid_add(t, 1, 1, m, PS[p][:, 0:n], PK(p))

        def fox_prompt_tile(t):
            c0, n, samp, qt = t
            ra = RA()
            SG = ra.get([128, 8, 512], BF16)
            base = ra.off
            ra2 = RA(); ra2.off = base

            def qsink(c8, qnb, key):
                for hh in range(2):
                    S.dma('sp', lambda e, hh=hh: e.dma_start(out=QD[2 * c8 + hh, 0:64, c0:c0 + n], in_=qnb[hh * 64:(hh + 1) * 64, :]),
                          reads=[key], writes=['QD'])
            qg_proj(t, ra2, qsink, SG)
            S.barrier()
            ra.off = base
            nkeys = NPR + (qt + 1) * 512
            nkb = nkeys // 128
            KHB = [ra.get([70, 4096], BF16) for _ in range(2)]
            VHB = [ra.get([128, 32, 128], BF16) for _ in range(2)]
            QHB = [ra.get([70, 512], BF16) for _ in range(2)]
            PTB = [ra.get([128, 512], BF16) for _ in range(4)]
            OS = ra.get([128, 512], F32); RD = ra.get([64, 512], F32); ONB = ra.get([64, 512], BF16)
            LP1 = ra.get([128, NPG, 16], F32)
            def issue_loads(h):
                r = h % 2
                S.dma('sp', lambda e, h=h, r=r: e.dma_start(out=KHB[r][:, 0:nkeys], in_=KTD[h, :, 0:nkeys]), reads=['KTD'], writes=[('KHB', r)])
                S.dma('sp', lambda e, h=h, r=r: e.dma_start(out=VHB[r][:, 0:nkb, :], in_=VD[0:nkeys, h, :].rearrange("(b p) x -> p b x", p=128)),
                      reads=['VD'], writes=[('VHB', r)])
                S.dma('sp', lambda e, h=h, r=r: e.dma_start(out=QHB[r][:, :], in_=QD[h, :, c0:c0 + n]), reads=['QD'], writes=[('QHB', r)])
            issue_loads(0)
            for h in range(16):
                r = h % 2
                kk, vk, qk = ('KHB', r), ('VHB', r), ('QHB', r)
                if h + 1 < 16:
                    issue_loads(h + 1)
                bq = qt * 4 + h // 4
                emit_R_gather(bq, range((h % 4) * 4, (h % 4) * 4 + 4), LP1)
                if h % 4 == 3:
                    emit_R_compute(bq, LP1)
                oa = 4 + h % 2

                def emit_st(kb, h=h, r=r, kk=kk, qk=qk):
                    diag = kb - (16 + 4 * qt)
                    col0 = max(0, diag) * 128
                    sp_ = kb % 4
                    mms = [(PS[sp_][:, col0:512], KHB[r][:, kb * 128:(kb + 1) * 128], QHB[r][:, col0:512], True, diag < 0)]
                    if diag >= 0:
                        mms.append((PS[sp_][:, col0:col0 + 128], IDB, NEGMB, False, True))
                    pe_group(mms, [kk, qk, 'CB'], [PK(sp_)])
                    act(PTB[sp_][:, col0:512], PS[sp_][:, col0:512], AF.Exp, [PK(sp_)], [('PTB', sp_)])

                def emit_pv(kb, h=h, r=r, vk=vk, oa=oa):
                    diag = kb - (16 + 4 * qt)
                    col0 = max(0, diag) * 128
                    sp_ = kb % 4
                    pe_group([(PS[oa][:, col0:512], VHB[r][:, kb, :], PTB[sp_][:, col0:512], kb == 0, kb == nkb - 1)], [vk, ('PTB', sp_)], [PK(oa)])
                LOOK = 2
                for kb in range(nkb + LOOK):
                    if kb < nkb:
                        emit_st(kb)
                    if kb >= LOOK:
                        emit_pv(kb - LOOK)
                act(OS[:, :], PS[oa][:, 0:512], AF.Copy, [PK(oa)], ['OS'])
                if qt == 0 and h == 0:
                    dbg("os", OS[:, :], [128, 512], ['OS'])
                pe_group([(PS[6][0:64, 0:512], cf(C_SHIFT)[:, 0:64], OS[:, :], True, True)], ['OS', 'CF'], [PK(6)])
                dve(lambda e: e.reciprocal(out=RD[:, :], in_=PS[6][0:64, 0:512]), [PK(6)], ['RD'])
                if qt == 0 and h == 0:
                    dbg("rd", RD[:, :], [64, 512], ['RD'])
                dve(lambda e: e.tensor_tensor(out=ONB[:, :], in0=OS[0:64, :], in1=RD[:, :], op=ALU.mult), ['OS', 'RD'], ['ONB'])
                S.dma('sp', lambda e, h=h: e.dma_start(out=OD[h * 64:(h + 1) * 64, c0:c0 + n], in_=ONB[:, :]), reads=['ONB'], writes=['OD'])
            S.barrier()
            ra.off = base
            OG = ra.get([128, 8, 512], BF16)
            S.dma('sp', lambda e: e.dma_start(out=OG[:, :, :], in_=OD[:, c0:c0 + n].rearrange("(c p) n -> p c n", p=128)), reads=['OD'], writes=['OG'])
            dve(lambda e: e.tensor_tensor(out=OG[:], in0=OG[:], in1=SG[:], op=ALU.mult), ['OG', 'SG'], ['OG'])
            wo_proj(t, OG)
            S.barrier()

        def fox_sample():
            t = S_TILE
            n = NS
            ra = RA()
            SGS = ra.get([128, 8, NS], BF16)
            QBD = ra.get([128, NSB, 8, 8], BF16)
            dve(lambda e: e.memset(QBD[:], 0.0), [], ['QBD'])

            def qsink(c8, qnb, key):
                for hh in range(2):
                    ps_ = slice(hh * 64, (hh + 1) * 64)
                    dve(lambda e, ps_=ps_, hh=hh: e.tensor_copy(out=QBD[ps_, :, c8, hh * 4:(hh + 1) * 4],
                                                                in_=qnb[ps_, :].rearrange("p (b t) -> p b t", t=4)), [key], ['QBD'])
            base0 = ra.off
            ra2 = RA(); ra2.off = base0
            qg_proj(t, ra2, qsink, SGS)
            S.barrier()
            ra.off = base0
            IDX = IDXG
            MKN = ra.get([64, 1024], F32); BIASN = MKN; NCN = ra.get([64, 16], F32)
            KVP = [ra.get([128, 2, 2048], BF16) for _ in range(2)]
            KTt = [ra.get([128, 8, 2, 128], BF16) for _ in range(2)]
            SBt = ra.get([128, 128], F32); PTp = [ra.get([128, 2, 64], BF16) for _ in range(3)]
            SNB = ra.get([64, 64], F32); PTN = ra.get([64, 64], BF16)
            off3 = ra.off
            KVP.append(carve(off3, [128, 2, 2048], BF16))
            KP = [b_[:, :, 0:1024] for b_ in KVP]
            VP = [b_[:, :, 1024:2048] for b_ in KVP]
            assert ra.off <= off3
            ra.off = off3
            OF = ra.get([128, 1024], F32); RDS = ra.get([128, 1024], F32)
            OGS = ra.get([128, 8, NS], BF16)
            S.dma('sp', lambda e: e.dma_start(out=MKN[:, :], in_=maskn[:, :]), writes=['MKN'])
            pe_group([(PS[3][0:64, 0:16], cf(C_TRI1S)[0:64, 0:64], LFT[0:64, 32, :], True, True)], ['LFT', 'CF'], [PK(3)])
            dve(lambda e: e.tensor_scalar(out=NCN[:, :], in0=PS[3][0:64, 0:16], scalar1=-1.0, scalar2=None, op0=ALU.mult), [PK(3)], ['NCN'])
            dve(lambda e: e.tensor_tensor(out=BIASN[:, :].rearrange("p (b h t) -> p b h t", b=16, h=16), in0=MKN[:, :].rearrange("p (b h t) -> p b h t", b=16, h=16),
                                          in1=NCN[:, :].unsqueeze(1).unsqueeze(3).to_broadcast([64, 16, 16, 4]), op=ALU.add), ['MKN', 'NCN'], ['MKN', 'BIASN'])
            def stage_A(b, pp):
                gi = b * (NPG // 2) + pp
                r = gi % 3
                r2 = gi % 2
                kk, vk = ('KP', r), ('VP', r)
                RB = rb_view(b)
                for pg in range(2):
                    i = b * NPG + pp * 2 + pg
                    S.dma('pool', lambda e, i=i, pg=pg, r=r: e.indirect_dma_start(out=KVP[r][:, pg, :], out_offset=None, in_=ckv[:, :],
                                                                                  in_offset=bass.IndirectOffsetOnAxis(ap=IDX[:, i:i + 1], axis=0)),
                          reads=['IDX'], writes=[kk, vk])
                tk = ('KTt', r2)
                for g in range(4):
                    pb_ = g % 2
                    mms = []
                    for cc in range(2):
                        for pg in range(2):
                            c8 = 2 * g + cc
                            mms.append((PS[pb_][:, (cc * 2 + pg) * 128:(cc * 2 + pg + 1) * 128], KP[r][:, pg, c8 * 128:(c8 + 1) * 128], IDB, True, True))
                    pe_group(mms, [kk, 'CB'], [PK(pb_)])
                    if g % 2 == 0:
                        act(KTt[r2][:, 2 * g:2 * g + 2, :, :].rearrange("p a b c -> p (a b c)"), PS[pb_][:, 0:512], AF.Copy, [PK(pb_)], [tk])
                    else:
                        dve(lambda e, g=g, pb_=pb_, r=r: e.tensor_copy(out=KTt[r2][:, 2 * g:2 * g + 2, :, :].rearrange("p a b c -> p (a b c)"), in_=PS[pb_][:, 0:512]),
                            [PK(pb_)], [tk], big=True)
                mms = []
                for pg in range(2):
                    for c8 in range(8):
                        mms.append((PS[2][:, pg * 64 + c8 * 8:pg * 64 + (c8 + 1) * 8], KTt[r2][:, c8, pg, :], QBD[:, b, c8, :], True, True))
                pe_group(mms, [tk, 'QBD'], [PK(2)])
                dve(lambda e, pp=pp, RB=RB: e.tensor_tensor(out=SBt[:, :].rearrange("p (a h t) -> p a h t", a=2, h=16),
                                                            in0=PS[2][:, 0:128].rearrange("p (a h t) -> p a h t", a=2, h=16),
                                                            in1=RB[:, 2 * pp:2 * pp + 2, :].unsqueeze(3).to_broadcast([128, 2, 16, 4]), op=ALU.add),
                    [PK(2), rb_key(b)], ['SBt'])
                act(PTp[r][:].rearrange("p a x -> p (a x)"), SBt[:, :], AF.Exp, ['SBt'], [('PTp', r)])

            def stage_B(b, pp):
                gi = b * (NPG // 2) + pp
                r = gi % 3
                vk, pk = ('VP', r), ('PTp', r)
                ob = 4 + b // 8
                db = 6 + b // 8
                bo = (b % 8) * 64
                mms = []
                for pg in range(2):
                    first = (pp == 0 and pg == 0)
                    for c8 in range(8):
                        mms.append((PS[ob][:, bo + c8 * 8:bo + (c8 + 1) * 8], VP[r][:, pg, c8 * 128:(c8 + 1) * 128], PTp[r][:, pg, c8 * 8:(c8 + 1) * 8],
                                    first and c8 == 0, False))
                    mms.append((PS[db][:, bo:bo + 64], ONEB, PTp[r][:, pg, :], first, False))
                pe_group(mms, [vk, pk, 'CB'], [PK(ob), PK(db)])
                if pp != NPG // 2 - 1:
                    return
                pe_group([(PS[3][0:64, c8 * 8:(c8 + 1) * 8], KNS[:, c8, :], QBD[:, b, c8, :], True, True) for c8 in range(8)], ['KNS', 'QBD'], [PK(3)])
                dve(lambda e, b=b: e.tensor_tensor(out=SNB[:, :], in0=PS[3][0:64, 0:64], in1=BIASN[:, b * 64:(b + 1) * 64], op=ALU.add), [PK(3), 'BIASN'], ['SNB'])
                act(PTN[:, :], SNB[:, :], AF.Exp, ['SNB'], ['PTN'])
                mms = []
                for c8 in range(8):
                    mms.append((PS[ob][:, bo + c8 * 8:bo + (c8 + 1) * 8], VNS[:, c8 * 128:(c8 + 1) * 128], PTN[:, c8 * 8:(c8 + 1) * 8], False, True))
                mms.append((PS[db][:, bo:bo + 64], ONEB[0:64, :], PTN[:, :], False, True))
                pe_group(mms, ['VNS', 'PTN', 'CB'], [PK(ob), PK(db)])

            items = [(b, pp) for b in range(NSB) for pp in range(NPG // 2)]
            for i, (b, pp) in enumerate(items):
                stage_A(b, pp)
                if i >= 1:
                    stage_B(*items[i - 1])
            stage_B(*items[-1])
            for hb in range(2):
                dve(lambda e, hb=hb: e.reciprocal(out=RDS[:, hb * 512:(hb + 1) * 512], in_=PS[6 + hb][:, 0:512]), [PK(6 + hb)], ['RDS', ('VP', 2)])
                dve(lambda e, hb=hb: e.tensor_tensor(out=OF[:, hb * 512:(hb + 1) * 512], in0=PS[4 + hb][:, 0:512], in1=RDS[:, hb * 512:(hb + 1) * 512], op=ALU.mult),
                    [PK(4 + hb), 'RDS'], ['OF', ('KP', 2)])
            for hh in range(2):
                ps_ = slice(hh * 64, (hh + 1) * 64)
                dve(lambda e, ps_=ps_, hh=hh: e.tensor_copy(out=OGS[ps_, :, :].rearrange("p c (b t) -> p b c t", t=4),
                                                            in_=OF[ps_, :].rearrange("p (b c x) -> p b c x", b=16, c=8)[:, :, :, hh * 4:(hh + 1) * 4]),
                    ['OF'], ['OG'])
            dve(lambda e: e.tensor_tensor(out=OGS[:], in0=OGS[:], in1=SGS[:], op=ALU.mult), ['OG', 'SG'], ['OG'])
            wo_proj(t, OGS)
            S.barrier()

        def load_x(src, c0, n, tiles):
            for k in range(KC):
                for t in tiles:
                    S.dma('sp', lambda e, k=k, t=t: e.dma_start(out=H[:, k, t[0]:t[0] + t[1]], in_=src[k * 128:(k + 1) * 128, t[0] - c0:t[0] - c0 + t[1]]),
                          writes=[hk(t)])

        G0 = [PT_TILES[0], PT_TILES[1]]
        G1 = [PT_TILES[2], PT_TILES[3]]
        G1S = [PT_TILES[2], PT_TILES[3], S_TILE]
        load_x(xp, 0, NPR, PT_TILES)
        dve(lambda e: e.memset(SST[:], 0.0), [], ['SST'])
        act(SBF[:], SST[:], AF.Copy, ['SST'], ['SBF'])
        ffn(0, 0, 0, G0); ffn(0, 0, 0, G1)
        for t in PT_TILES:
            gla_tile(t, bar=(t is PT_TILES[-1]))
        ffn(0, 1, 2, G0); ffn(0, 1, 2, G1)
        for t in PT_TILES:
            kv_tile(t, 0, False, bar=(t is PT_TILES[-1]))
        for h in range(4):
            dve(lambda e, h=h: e.tensor_scalar(out=SST[:, h, :], in0=SST[:, h, :], scalar1=FLG[:, 0:1], scalar2=None, op0=ALU.mult), ['SST', 'small'], ['SST'])
        act(SBF[:], SST[:], AF.Copy, ['SST'], ['SBF'])
        load_x(xo, 0, NPR, PT_TILES)
        load_x(xs, NPR, NS, [S_TILE])
        ffn(0, 0, 0, G0); ffn(0, 0, 0, G1S)
        dbg_h("l0_ffn0")
        for t in PT_TILES:
            gla_tile(t, bar=(t is PT_TILES[-1]))
        S.dma('sp', lambda e: e.dma_start(out=sgp.rearrange("h d e -> d h e"), in_=SST[:, :, :]), reads=['SST'])
        gla_tile(S_TILE)
        dbg_h("l0_mix")
        ffn(0, 1, 2, G0); ffn(0, 1, 2, G1S)
        dbg_h("l0_ffn1")
        for t in PT_TILES:
            kv_tile(t, NPR, True, bar=(t is PT_TILES[-1]))
        kv_tile(S_TILE, 0, True)
        f_pipeline()
        S.dma('sp', lambda e: e.dma_start(out=IDXG[:, :], in_=ptab[0:1, :].partition_broadcast(128)), reads=['MOD'], writes=['IDX', 'MOD'])
        S.dma('sp', lambda e: e.dma_start(out=IOTG[:, :], in_=iota[:, :]), reads=['small'], writes=['IOT', 'small'])
        dve(lambda e: e.tensor_scalar(out=IDXG[:, :], in0=IDXG[:, :], scalar1=128, scalar2=IOTG[:, 0:1], op0=ALU.mult, op1=ALU.add), ['IDX', 'IOT'], ['IDX'])
        ffn(1, 0, 0, G0); ffn(1, 0, 0, G1S)
        dbg_h("l1_ffn0")
        wb_lim[0] = 2
        wbi[0] = 0
        for t in PT_TILES:
            fox_prompt_tile(t)
        fox_sample()
        wb_lim[0] = len(WB)
        dbg_h("l1_mix")
        ffn(1, 1, 2, G0); ffn(1, 1, 2, G1S)
        for k in range(KC):
            S.dma('sp', lambda e, k=k: e.dma_start(out=yo[k * 128:(k + 1) * 128, :], in_=H[:, k, 0:NPR]), reads=[hk(t) for t in PT_TILES])
            S.dma('sp', lambda e, k=k: e.dma_start(out=ys[k * 128:(k + 1) * 128, :], in_=H[:, k, NPR:NT]), reads=[hk(S_TILE)])
        S.finish()
        S.emit(block)
    return nc


def _shared_inputs(inp):
    f = lambda a: np.ascontiguousarray(np.asarray(a, dtype=np.float32))
    b_ada = np.asarray(inp['b_ada'], np.float32)
    g_norm = np.asarray(inp['g_norm'], np.float32)
    sh = {
        'w_ada': f(inp['w_ada']),
        'b_adaT': f(np.stack([b_ada[l].reshape(72, 128).T for l in range(2)], axis=1)),
        'g_normT': f(np.stack([np.stack([g_norm[l, s].reshape(8, 128).T for s in range(3)], axis=1) for l in range(2)], axis=1)),
        'w_up': f(inp['w_ffn_up']), 'w_dn': f(inp['w_ffn_down']),
        'w_in': f(np.asarray(inp['gla_w_in'])[0]),
        'wg2b': f(np.concatenate([np.asarray(inp['gla_w_gate2'])[0], np.asarray(inp['gla_b_gate'])[0][None, :]], axis=0)),
        'goutT': f(np.asarray(inp['gla_g_out'])[0].reshape(2, 128).T),
        'w_out': f(np.asarray(inp['gla_w_out'])[0]),
        'w_akv': f(inp['w_ada_kv']),
        'b_akvT': f(np.asarray(inp['b_ada_kv']).reshape(16, 128).T),
        'g_kvT': f(np.asarray(inp['g_kv']).reshape(8, 128).T),
        'w_kvf': f(inp['w_kvf']),
        'bfb': f(np.tile(np.asarray(inp['b_f'])[None, :], (128, 1))),
        'gk2': f(np.tile(np.asarray(inp['g_k']), 2)[:, None]),
        'w_qg': f(np.asarray(inp['fox_w_qg'])[0]),
        'gq2': f(np.tile(np.asarray(inp['fox_g_q'])[0], 2)[:, None]),
        'w_o': f(np.asarray(inp['fox_w_o'])[0]),
        'cpack': _consts(), 'maskn': _maskn(),
        'iota': np.arange(128, dtype=np.int32)[:, None],
    }
    return sh


def _core_inputs(inp, c, shared, ckv, clf, page_table):
    f = lambda a: np.ascontiguousarray(np.asarray(a, dtype=np.float32))
    b, half = c // 2, c % 2
    xpr = np.asarray(inp['x_prompt'])
    d = dict(shared)
    d['xo'] = f(xpr[b, half * NPR:(half + 1) * NPR].T)
    d['xp'] = f(xpr[b, 0:NPR].T) if half == 1 else np.zeros((D, NPR), np.float32)
    d['xs'] = f(np.asarray(inp['x_sample'])[NSB * c:NSB * (c + 1)].reshape(NS, D).T)
    d['cT'] = f(np.concatenate([np.asarray(inp['c_prompt'])[b][None, :], np.asarray(inp['c_sample'])[NSB * c:NSB * (c + 1)]], axis=0).T)
    d['st0'] = f(np.asarray(inp['state_gla'])[0, NSB * c:NSB * (c + 1)])
    d['ckv'] = ckv
    d['clf'] = clf
    d['ptab'] = np.ascontiguousarray(np.asarray(page_table)[NSB * c:NSB * (c + 1)].reshape(1, NSB * NPG).astype(np.int32))
    d['flg'] = np.ascontiguousarray(np.tile(np.array([[float(half), BIG * (1.0 - half)]], np.float32), (128, 1)))
    return d


def _assemble(results, cores, nb_prompt, nb_sample):
    y_p = np.zeros((nb_prompt, 2 * NPR, D), np.float32); y_s = np.zeros((nb_sample, 4, D), np.float32)
    sg_p = np.zeros((1, nb_prompt, 4, 128, 256), np.float32)
    k_p = np.zeros((nb_prompt, 2 * NPR, 16, 64), np.float32); v_p = np.zeros_like(k_p)
    lf_p = np.zeros((nb_prompt, 2 * NPR, 16), np.float32)
    sg_s = np.zeros((1, nb_sample, 4, 128, 256), np.float32)
    k_s = np.zeros((nb_sample, 4, 16, 64), np.float32); v_s = np.zeros_like(k_s)
    lf_s = np.zeros((nb_sample, 4, 16), np.float32)
    for c, r in zip(cores, results):
        b, half = c // 2, c % 2
        sl = slice(half * NPR, (half + 1) * NPR)
        ss = slice(NSB * c, NSB * (c + 1))
        y_p[b, sl] = r['yo'].T
        y_s[ss] = r['ys'].T.reshape(NSB, 4, D)
        if half == 1:
            sg_p[0, b] = r['sgp']
        k_p[b, sl] = r['ko'].T.reshape(NPR, 16, 64)
        v_p[b, sl] = r['vo'].reshape(NPR, 16, 64)
        lf_p[b, sl] = r['lfo']
        sg_s[0, ss] = r['sgs']
        k_s[ss] = r['kso'].T.reshape(NSB, 4, 16, 64)
        v_s[ss] = r['vso'].reshape(NSB, 4, 16, 64)
        lf_s[ss] = r['lfso'].reshape(NSB, 4, 16)
    return (y_p, y_s, sg_p, k_p, v_p, lf_p, sg_s, k_s, v_s, lf_s)


def kernel(**inputs):
    cache_k = np.asarray(inputs['cache_k'], np.float32)
    n_phys = cache_k.shape[0]
    ckv = np.concatenate([cache_k.reshape(n_phys * 128, D), np.asarray(inputs['cache_v'], np.float32).reshape(n_phys * 128, D)], axis=1)
    clf = np.ascontiguousarray(np.asarray(inputs['cache_logf'], np.float32).reshape(n_phys * 128, 16))
    shared = _shared_inputs(inputs)
    cores = list(range(N_CORES))
    in_maps = [_core_inputs(inputs, c, shared, ckv, clf, inputs['page_table']) for c in cores]
    nc = build_program(n_phys)
    res = run_bass_kernel_spmd(nc, in_maps, core_ids=cores)
    return _assemble(res.results, cores, 4, 128)
```
